# Optimizing a Trainium2 kernel written in Bass

```python
import math
import jax, jax.numpy as jnp
from jax import lax
import numpy as np

D_MODEL = 1024
BATCH = 8
SEQ = 8192
DEPTH = 4

N_GROUPS = 4
GROUP_WIDTH = D_MODEL // N_GROUPS
HEADS = 4
HEAD_DIM = GROUP_WIDTH // HEADS
N_SLICES = 11
IN_WIDTH = N_SLICES * GROUP_WIDTH
CHUNK = 128
ROPE_BASE = 10000.0
LRU_C = 8.0
LRU_CONV_WIDTH = 4
LRU_CONV_PAD = (2, 1)
SC_CONV_WIDTH = 3
SC_CONV_PAD = (1, 1)
D_FF = math.ceil(8 * D_MODEL / 3 / 256) * 256
DEEPNORM_ALPHA = (2 * DEPTH) ** 0.25
DEEPNORM_BETA = (8 * DEPTH) ** -0.25
LN_EPS = 1e-5

kernel_name = 'hybrid_parallel_group_encoder'


def layer_norm(x, g, b):
    xf = x.astype(jnp.float32)
    mu = xf.mean(-1, keepdims=True)
    var = jnp.square(xf - mu).mean(-1, keepdims=True)
    return ((xf - mu) * lax.rsqrt(var + LN_EPS) * g + b).astype(x.dtype)


def depthwise_conv(x, w, pad):
    return lax.conv_general_dilated(
        x, w[:, None, :], window_strides=(1,), padding=[pad],
        dimension_numbers=('NWC', 'WIO', 'NWC'), feature_group_count=x.shape[-1])


def gmlp_mixer(u, v, ln_g, ln_b, w_s, b_s):
    bsz, s, w = u.shape
    u = jax.nn.gelu(u)
    v = layer_norm(jax.nn.gelu(v), ln_g, ln_b)
    vc = v.reshape(bsz, s // CHUNK, CHUNK, HEADS, HEAD_DIM)
    mixed = jnp.einsum('hpq,bnqhd->bnphd', w_s, vc) + b_s.T[None, None, :, :, None]
    return u * mixed.reshape(bsz, s, w)


def rotary(x, cos, sin):
    x1, x2 = jnp.split(x, 2, axis=-1)
    return jnp.concatenate([x1 * cos - x2 * sin, x2 * cos + x1 * sin], axis=-1)


def retention_mixer(q, k, v, g, positions, gn_g, gn_b):
    bsz, s, w = q.shape
    nc, f32 = s // CHUNK, jnp.float32
    inv_freq = ROPE_BASE ** (-jnp.arange(0, HEAD_DIM, 2, dtype=f32) / HEAD_DIM)
    ang = positions.astype(f32)[..., None] * inv_freq
    cos, sin = jnp.cos(ang)[:, :, None, :], jnp.sin(ang)[:, :, None, :]
    shp = (bsz, s, HEADS, HEAD_DIM)
    qh = rotary(q.astype(f32).reshape(shp), cos, sin)
    kh = rotary(k.astype(f32).reshape(shp), cos, sin) * HEAD_DIM ** -0.5
    vh = v.astype(f32).reshape(shp)
    cshp = (bsz, nc, CHUNK, HEADS, HEAD_DIM)
    qc, kc, vc = qh.reshape(cshp), kh.reshape(cshp), vh.reshape(cshp)
    log_gamma = jnp.log1p(-jnp.exp2(-5.0 - jnp.arange(HEADS, dtype=f32)))
    idx = jnp.arange(CHUNK, dtype=f32)
    intra = jnp.exp(log_gamma[:, None, None] * jnp.abs(idx[:, None] - idx[None, :]))
    scores = jnp.einsum('bnihd,bnjhd->bnhij', qc, kc) * intra
    out = jnp.einsum('bnhij,bnjhe->bnihe', scores, vc)
    k_dec_f = jnp.exp(log_gamma[None, :] * (CHUNK - 1 - idx)[:, None])
    k_dec_b = jnp.exp(log_gamma[None, :] * idx[:, None])
    u_f = jnp.einsum('bnjhd,bnjhe->nbhde', kc * k_dec_f[None, None, :, :, None], vc)
    u_b = jnp.einsum('bnjhd,bnjhe->nbhde', kc * k_dec_b[None, None, :, :, None], vc)
    chunk_decay = jnp.exp(log_gamma * CHUNK)[None, :, None, None]

    def step(state, u):
        return chunk_decay * state + u, state

    zero = jnp.zeros((bsz, HEADS, HEAD_DIM, HEAD_DIM), f32)
    _, r_prev = lax.scan(step, zero, u_f)
    _, l_next = lax.scan(step, zero, u_b, reverse=True)
    q_dec_f = jnp.exp(log_gamma[None, :] * (idx + 1.0)[:, None])
    q_dec_b = jnp.exp(log_gamma[None, :] * (CHUNK - idx)[:, None])
    out = (out
           + jnp.einsum('bnihd,nbhde->bnihe', qc * q_dec_f[None, None, :, :, None], r_prev)
           + jnp.einsum('bnihd,nbhde->bnihe', qc * q_dec_b[None, None, :, :, None], l_next))
    out = out.reshape(shp)
    mu = out.mean(-1, keepdims=True)
    var = jnp.square(out - mu).mean(-1, keepdims=True)
    out = ((out - mu) * lax.rsqrt(var + LN_EPS)).reshape(bsz, s, w) * gn_g + gn_b
    return (jax.nn.silu(g.astype(f32)) * out).astype(q.dtype)


def linear_scan(a, b):
    def combine(left, right):
        a_l, b_l = left
        a_r, b_r = right
        return a_l * a_r, a_r * b_l + b_r
    _, h = lax.associative_scan(combine, (a, b), axis=1)
    return h


def rglru_mixer(xr, gate, conv_w, conv_b, w_a, b_a, w_x, b_x, lam):
    bsz, s, w = xr.shape
    f32 = jnp.float32
    xf = (depthwise_conv(xr, conv_w, LRU_CONV_PAD) + conv_b).astype(f32)
    xb = xf.reshape(bsz, s, HEADS, HEAD_DIM)
    r = jax.nn.sigmoid(jnp.einsum('bshi,zhij->zbshj', xb, w_a).reshape(2, bsz, s, w) + b_a[:, None, None, :])
    i = jax.nn.sigmoid(jnp.einsum('bshi,zhij->zbshj', xb, w_x).reshape(2, bsz, s, w) + b_x[:, None, None, :])
    log_a = -LRU_C * jax.nn.softplus(-lam.astype(f32))[:, None, None, :] * r
    a = jnp.exp(log_a)
    b = xf[None] * i * jnp.sqrt(-jnp.expm1(2.0 * log_a))
    h_fwd = linear_scan(a[0], b[0])
    h_bwd = jnp.flip(linear_scan(jnp.flip(a[1], 1), jnp.flip(b[1], 1)), 1)
    return ((h_fwd + h_bwd) * jax.nn.gelu(gate.astype(f32))).astype(gate.dtype)


def short_conv_mixer(bg, cg, h, conv_w):
    return bg * depthwise_conv(cg * h, conv_w, SC_CONV_PAD)


def setup_inputs(seed: int = 0) -> dict:
    key = jax.random.key(seed)
    ks = jax.random.split(key, 24)
    L, D, W = DEPTH, D_MODEL, GROUP_WIDTH
    f32 = jnp.float32
    nrm = lambda k, shape, scale: jax.random.normal(k, shape, f32) * scale
    gain = lambda k, shape: 1.0 + 0.02 * jax.random.normal(k, shape, f32)
    u = jax.random.uniform(ks[12], (L, 2, W), f32, 0.9, 0.999)
    sig = u ** (1.0 / LRU_C)
    lru_lambda = jnp.log(sig) - jnp.log1p(-sig)
    return {
        'x': jax.random.normal(ks[0], (BATCH, SEQ, D), f32),
        'positions': jnp.broadcast_to(jnp.arange(SEQ, dtype=jnp.int32), (BATCH, SEQ)),
        'w_in': nrm(ks[1], (L, D, IN_WIDTH), D ** -0.5),
        'gmlp_ln_g': gain(ks[2], (L, W)),
        'gmlp_ln_b': nrm(ks[3], (L, W), 0.02),
        'gmlp_ws': nrm(ks[4], (L, HEADS, CHUNK, CHUNK), CHUNK ** -0.5),
        'gmlp_bs': gain(ks[5], (L, HEADS, CHUNK)),
        'ret_gn_g': gain(ks[6], (L, W)),
        'ret_gn_b': nrm(ks[7], (L, W), 0.02),
        'lru_conv_w': nrm(ks[8], (L, LRU_CONV_WIDTH, W), LRU_CONV_WIDTH ** -0.5),
        'lru_conv_b': nrm(ks[9], (L, W), 0.02),
        'lru_wa': nrm(ks[10], (L, 2, HEADS, HEAD_DIM, HEAD_DIM), HEAD_DIM ** -0.5),
        'lru_ba': nrm(ks[11], (L, 2, W), 0.02),
        'lru_wx': nrm(ks[13], (L, 2, HEADS, HEAD_DIM, HEAD_DIM), HEAD_DIM ** -0.5),
        'lru_bx': nrm(ks[14], (L, 2, W), 0.02),
        'lru_lambda': lru_lambda,
        'sc_conv_w': nrm(ks[15], (L, SC_CONV_WIDTH, W), SC_CONV_WIDTH ** -0.5),
        'w_out': nrm(ks[16], (L, D, D), D ** -0.5 * DEEPNORM_BETA),
        'ln1_g': gain(ks[17], (L, D)),
        'ln1_b': nrm(ks[18], (L, D), 0.02),
        'ffn_wg': nrm(ks[19], (L, D, D_FF), D ** -0.5),
        'ffn_wu': nrm(ks[20], (L, D, D_FF), D ** -0.5),
        'ffn_wd': nrm(ks[21], (L, D_FF, D), D_FF ** -0.5 * DEEPNORM_BETA),
        'ln2_g': gain(ks[22], (L, D)),
        'ln2_b': nrm(ks[23], (L, D), 0.02),
    }


def reference(x, positions, w_in, gmlp_ln_g, gmlp_ln_b, gmlp_ws, gmlp_bs, ret_gn_g, ret_gn_b,
              lru_conv_w, lru_conv_b, lru_wa, lru_ba, lru_wx, lru_bx, lru_lambda, sc_conv_w,
              w_out, ln1_g, ln1_b, ffn_wg, ffn_wu, ffn_wd, ln2_g, ln2_b):
    for l in range(DEPTH):
        z = jnp.einsum('bsd,dk->bsk', x, w_in[l])
        (a_u, a_v, r_q, r_k, r_v, r_g, c_x, c_gate, d_b, d_c, d_h) = jnp.split(z, N_SLICES, axis=-1)
        y = jnp.concatenate([
            gmlp_mixer(a_u, a_v, gmlp_ln_g[l], gmlp_ln_b[l], gmlp_ws[l], gmlp_bs[l]),
            retention_mixer(r_q, r_k, r_v, r_g, positions, ret_gn_g[l], ret_gn_b[l]),
            rglru_mixer(c_x, c_gate, lru_conv_w[l], lru_conv_b[l], lru_wa[l], lru_ba[l],
                        lru_wx[l], lru_bx[l], lru_lambda[l]),
            short_conv_mixer(d_b, d_c, d_h, sc_conv_w[l]),
        ], axis=-1)
        x = layer_norm(DEEPNORM_ALPHA * x + y @ w_out[l], ln1_g[l], ln1_b[l])
        f = (jax.nn.silu(x @ ffn_wg[l]) * (x @ ffn_wu[l])) @ ffn_wd[l]
        x = layer_norm(DEEPNORM_ALPHA * x + f, ln2_g[l], ln2_b[l])
    return x
```

```python
import math
from contextlib import ExitStack

import numpy as np
import concourse.bass as bass
import concourse.mybir as mybir
from concourse.ap import AP
from concourse.bass_utils import run_bass_kernel_spmd

F32 = mybir.dt.float32
BF16 = mybir.dt.bfloat16
I32 = mybir.dt.int32
AF = mybir.ActivationFunctionType
ALU = mybir.AluOpType

D = 1024
S = 8192
L = 4
W = 256
NIN = 2816
DFF = 2816
NCH = S // 128
TA = 256
TB = 512
TC = 256
ALPHA = float((2 * L) ** 0.25)
EPS = 1e-5
NCORES = 8
TWO_PI = float(np.float32(2 * math.pi))
C2PI = float(2 * math.pi - float(np.float32(2 * math.pi)))
PI_LO = 3.1415925
S1_ROUND = 9
C_HI = 6.28125
C_LO = 2 * math.pi - 6.28125


class Buf:
    __slots__ = ("name", "w", "r", "excl", "strict")

    def __init__(self, name, excl=False):
        self.strict = False
        self.name = name
        self.w = None
        self.r = {}
        self.excl = excl


class Sched:
    NSLOT = 8

    def __init__(self, nc, es):
        self.nc = nc
        self.eng = {"pe": nc.tensor, "act": nc.scalar, "dve": nc.vector, "pool": nc.gpsimd, "sp": nc.sync}
        self.sem = {}
        self.cnt = {}
        self.semobj = {}
        for e in self.eng:
            s = es.enter_context(nc.semaphore("s_" + e))
            self.sem[e] = e
            self.semobj[e] = s
            self.cnt[e] = 0
        self.slots = {}
        self.slot_i = {}
        for q in ("sp", "pool"):
            self.slots[q] = []
            for i in range(self.NSLOT):
                key = "d_%s%d" % (q, i)
                self.semobj[key] = es.enter_context(nc.semaphore(key))
                self.cnt[key] = 0
                self.slots[q].append(key)
            self.slot_i[q] = 0
        self.known = {e: {} for e in self.eng}
        self.dead = False

    def _wait(self, e, deps, sself=0):
        k = self.known[e]
        if sself > k.get(e, 0):
            self.eng[e].wait_ge(self.semobj[e], sself)
            k[e] = sself
        for s, v in deps.items():
            if s == e and e != "sp":
                continue
            if k.get(s, 0) < v:
                self.eng[e].wait_ge(self.semobj[s], v)
                k[s] = v

    @staticmethod
    def _deps(reads, writes):
        deps = {}

        def add(t):
            if t is not None and deps.get(t[0], 0) < t[1]:
                deps[t[0]] = t[1]
        for b in reads:
            add(b.w)
            if b.excl:
                for s, v in b.r.items():
                    add((s, v))
        for b in writes:
            add(b.w)
            for s, v in b.r.items():
                add((s, v))
        return deps

    @staticmethod
    def _mark(t, reads, writes):
        for b in writes:
            b.w = t
            b.r = {}
        for b in reads:
            if b.excl:
                b.w = t
                b.r = {}
            elif b.r.get(t[0], 0) < t[1]:
                b.r[t[0]] = t[1]

    def op(self, e, fn, reads=(), writes=(), sync_self=False):
        if self.dead:
            return
        sself = 0
        for b in list(reads) + list(writes):
            if b.strict:
                if b.w is not None and b.w[0] == e:
                    sself = max(sself, b.w[1])
                sself = max(sself, b.r.get(e, 0))
        self._wait(e, self._deps(reads, writes), sself)
        if sync_self and self.cnt[e] > 0:
            self.eng[e].wait_ge(self.semobj[e], self.cnt[e])
        ins = fn()
        self.cnt[e] += 1
        ins.then_inc(self.semobj[e], 1)
        self._mark((e, self.cnt[e]), reads, writes)

    def dma(self, q, fns, reads=(), writes=()):
        if self.dead:
            return
        deps = self._deps(reads, writes)
        slot = self.slots[q][self.slot_i[q] % self.NSLOT]
        self.slot_i[q] += 1
        if self.cnt[slot] > 0:
            deps[slot] = max(deps.get(slot, 0), self.cnt[slot])
        self._wait(q, deps)
        for fn in fns:
            fn(self.eng[q]).then_inc(self.semobj[slot], 16)
        self.cnt[slot] += 16 * len(fns)
        self._mark((slot, self.cnt[slot]), reads, writes)

    def barrier(self):
        if self.dead:
            return
        allv = {s: v for s, v in self.cnt.items() if v > 0}
        for e in self.eng:
            k = self.known[e]
            for s, v in allv.items():
                if s == e:
                    continue
                if k.get(s, 0) < v:
                    self.eng[e].wait_ge(self.semobj[s], v)
                    k[s] = v


class Tl:
    def __init__(self, t, name, nb=1, excl=False):
        self.t = t
        self.b = Buf(name, excl)
        self.bs = [self.b] * nb if excl else [self.b] + [Buf(name + str(i)) for i in range(1, nb)]

    def __getitem__(self, k):
        return self.t[k]


def fap(t, sl, dims):
    base = t[sl]
    return AP(base.tensor, base.offset, [list(base.ap[0])] + [list(d) for d in dims])


class _Stop(Exception):
    pass


def build_program(n_layers=L, debug=False, stage=None):
    nc = bass.Bass("TRN2", target_bir_lowering=False)
    dk = "ExternalOutput" if debug else "Internal"

    def din(name, shape, dt=F32):
        return nc.dram_tensor(name, list(shape), dt, kind="ExternalInput").ap()

    x_in = din("x", [S, D])
    pos_in = din("pos", [128, NCH], I32)
    w_in_d = din("w_in", [L, D, NIN])
    w_out_d = din("w_out", [L, D, D])
    wg_d = din("ffn_wg", [L, D, DFF])
    wu_d = din("ffn_wu", [L, D, DFF])
    wd_d = din("ffn_wd", [L, DFF, D])
    pp_d = din("pp", [L, 128, 28])
    wbd_d = din("wbd", [L, 8, 128, 128])
    tb_d = din("tb", [L, 4 * W])
    bs_d = din("gmlp_bs", [L, 4, 128])
    ws_d = din("gmlp_ws", [L, 4, 128, 128])
    lnp_d = din("lnp", [L, 4 * D])
    ident_d = din("ident", [128, 128])
    mask_d = din("maskT", [128, 512])
    dec_d = din("dec", [128, 16])
    cdec_d = din("cdec", [128, 256])
    invf_d = din("invf", [128, 32])
    out_d = nc.dram_tensor("out", [S, D], F32, kind="ExternalOutput").ap()

    xT_s = nc.dram_tensor("xT_s", [D, S + 4], BF16, kind=dk).ap()
    x_s = nc.dram_tensor("x_s", [S, D], F32, kind="Internal").ap()
    yT_s = nc.dram_tensor("yT_s", [D, S], BF16, kind=dk).ap()
    lru_s = nc.dram_tensor("lru_s", [4, W, S], F32, kind=dk).ap()
    x1_s = nc.dram_tensor("x1_s", [S, D], F32, kind=dk).ap()
    x1T_s = nc.dram_tensor("x1T_s", [D, S], BF16, kind=dk).ap()

    V = nc.vector
    A = nc.scalar
    G = nc.gpsimd
    PE = nc.tensor

    with ExitStack() as es0:
        sc = Sched(nc, es0)

        uid = [0]

        def sb(es, name, shape, dt=F32, nb=1):
            uid[0] += 1
            name = "%s_u%d" % (name, uid[0])
            return Tl(es.enter_context(nc.sbuf_tensor(name, list(shape), dt)), name, nb)

        def ps(es, name, shape, dt=F32, nb=1):
            uid[0] += 1
            name = "%s_u%d" % (name, uid[0])
            return Tl(es.enter_context(nc.psum_tensor(name, list(shape), dt)), name, nb, excl=True)

        def stop(name):
            if stage == name:
                sc.barrier()
                sc.dead = True

        try:
            ident = sb(es0, "ident", [128, 128], BF16)
            maskT = sb(es0, "maskT", [128, 512])
            dec = sb(es0, "dec", [128, 16])
            cdec = sb(es0, "cdec", [128, 256])
            cst = sb(es0, "cst", [128, 4])
            sc.op("dve", lambda: V.memset(cst[:, 0:1], EPS), writes=[cst.b])
            sc.op("dve", lambda: V.memset(cst[:, 1:2], 1.0), writes=[cst.b])
            sc.op("dve", lambda: V.memset(cst[:, 2:3], math.log(0.5)), writes=[cst.b])
            sc.dma("pool", [lambda q: q.dma_start(out=ident[:], in_=ident_d)], writes=[ident.b])
            sc.dma("sp", [lambda q: q.dma_start(out=maskT[:], in_=mask_d)], writes=[maskT.b])
            sc.dma("sp", [lambda q: q.dma_start(out=dec[:], in_=dec_d)], writes=[dec.b])
            sc.dma("sp", [lambda q: q.dma_start(out=cdec[:], in_=cdec_d)], writes=[cdec.b])

            def transposes(pt, src_fn, n, reads):
                def f():
                    for j in range(n):
                        i = PE.transpose(out=pt[:, j, :], in_=src_fn(j), identity=ident[:])
                    return i
                sc.op("pe", f, reads=list(reads) + [ident.b], writes=[pt.b])

            def make_tables(es, cosT, sinT):
                with ExitStack() as est:
                    invf = sb(est, "invf", [128, 32])
                    posi = sb(est, "posi", [128, NCH], I32)
                    posf = sb(est, "posf", [128, NCH])
                    ang = sb(est, "ang", [128, NCH, 32])
                    t1 = sb(est, "t1", [128, NCH, 32])
                    t2 = sb(est, "t2", [128, NCH, 32])
                    sc.dma("sp", [lambda q: q.dma_start(out=invf[:], in_=invf_d)], writes=[invf.b])
                    sc.dma("sp", [lambda q: q.dma_start(out=posi[:], in_=pos_in)], writes=[posi.b])
                    sc.op("dve", lambda: V.tensor_copy(out=posf[:], in_=posi[:]), reads=[posi.b], writes=[posf.b])
                    sc.op("dve", lambda: V.tensor_tensor(
                        out=ang[:], in0=posf[:].unsqueeze(2).to_broadcast([128, NCH, 32]),
                        in1=invf[:].unsqueeze(1).to_broadcast([128, NCH, 32]), op=ALU.mult),
                        reads=[posf.b, invf.b], writes=[ang.b])
                    ki = sb(est, "ki", [128, NCH, 32], I32)
                    sc.op("dve", lambda: V.tensor_scalar(out=t2[:], in0=ang[:], scalar1=1.0 / (2 * math.pi), scalar2=None, op0=ALU.mult),
                          reads=[ang.b], writes=[t2.b])
                    sc.op("dve", lambda: V.tensor_copy(out=ki[:], in_=t2[:]), reads=[t2.b], writes=[ki.b])
                    sc.op("dve", lambda: V.tensor_copy(out=t2[:], in_=ki[:]), reads=[ki.b], writes=[t2.b])
                    sc.op("dve", lambda: V.scalar_tensor_tensor(out=t1[:], in0=t2[:], scalar=-C_HI, in1=ang[:],
                                                                op0=ALU.mult, op1=ALU.add), reads=[t2.b, ang.b], writes=[t1.b])
                    sc.op("dve", lambda: V.scalar_tensor_tensor(out=t1[:], in0=t2[:], scalar=-C_LO, in1=t1[:],
                                                                op0=ALU.mult, op1=ALU.add), reads=[t2.b, t1.b], writes=[t1.b])
                    sc.op("dve", lambda: V.tensor_scalar(out=t2[:], in0=t1[:], scalar1=0.0, scalar2=None, op0=ALU.is_lt),
                          reads=[t1.b], writes=[t2.b])
                    sc.op("dve", lambda: V.scalar_tensor_tensor(out=t1[:], in0=t2[:], scalar=2 * math.pi, in1=t1[:],
                                                                op0=ALU.mult, op1=ALU.add), reads=[t2.b, t1.b], writes=[t1.b])

                    def sin_of(dst, src):
                        sc.op("dve", lambda: V.tensor_scalar(out=t2[:], in0=src[:], scalar1=-math.pi, scalar2=PI_LO,
                                                             op0=ALU.add, op1=ALU.min), reads=[src.b], writes=[t2.b])
                        sc.op("dve", lambda: V.tensor_scalar(out=t2[:], in0=t2[:], scalar1=-PI_LO, scalar2=None, op0=ALU.max),
                              reads=[t2.b], writes=[t2.b])
                        sc.op("act", lambda: A.activation(out=dst[:], in_=t2[:], func=AF.Sin), reads=[t2.b], writes=[dst.b])
                        sc.op("dve", lambda: V.tensor_scalar(out=dst[:], in0=dst[:], scalar1=-1.0, scalar2=None, op0=ALU.mult),
                              reads=[dst.b], writes=[dst.b])

                    sin_of(sinT, t1)
                    sc.op("dve", lambda: V.tensor_scalar(out=t1[:], in0=t1[:], scalar1=math.pi / 2, scalar2=None, op0=ALU.add),
                          reads=[t1.b], writes=[t1.b])
                    sc.op("dve", lambda: V.tensor_scalar(out=ang[:], in0=t1[:], scalar1=2 * math.pi, scalar2=None, op0=ALU.is_ge),
                          reads=[t1.b], writes=[ang.b])
                    sc.op("dve", lambda: V.scalar_tensor_tensor(out=t1[:], in0=ang[:], scalar=-2 * math.pi, in1=t1[:],
                                                                op0=ALU.mult, op1=ALU.add), reads=[ang.b, t1.b], writes=[t1.b])
                    sin_of(cosT, t1)
                    sc.barrier()

            with ExitStack() as es:
                zpad = sb(es, "zpad", [128, 8, 2], BF16)
                sc.op("dve", lambda: V.memset(zpad[:], 0.0), writes=[zpad.b])
                for c0 in (0, S + 2):
                    sc.dma("sp", [lambda q, c0=c0: q.dma_start(
                        out=xT_s[:, c0:c0 + 2].rearrange("(k p) c -> p k c", p=128), in_=zpad[:])], reads=[zpad.b])

                xb = [sb(es, "xb%d" % i, [128, 4, D], BF16) for i in range(2)]
                xst = [sb(es, "xst%d" % i, [128, 8, 512], BF16) for i in range(2)]
                ptr = [ps(es, "ptr%d" % i, [128, 8, 128], BF16) for i in range(2)]
                for m in range(S // 512):
                    t0 = m * 512
                    b = xb[m % 2]
                    st = xst[m % 2]
                    sc.dma("pool", [lambda q, b=b, t0=t0: q.dma_start(
                        out=b[:], in_=x_in[t0:t0 + 512, :].rearrange("(s p) d -> p s d", p=128))], writes=[b.b])
                    for s in range(4):
                        pt = ptr[s % 2]
                        transposes(pt, lambda j, b=b, s=s: b[:, s, j * 128:(j + 1) * 128], 8, [b.b])
                        if s % 2 == 0:
                            sc.op("dve", lambda pt=pt, st=st, s=s: V.tensor_copy(out=st[:, :, s * 128:(s + 1) * 128], in_=pt[:]),
                                  reads=[pt.b], writes=[st.b])
                        else:
                            sc.op("act", lambda pt=pt, st=st, s=s: A.copy(out=st[:, :, s * 128:(s + 1) * 128], in_=pt[:]),
                                  reads=[pt.b], writes=[st.b])
                    sc.dma("sp", [lambda q, st=st, t0=t0: q.dma_start(
                        out=xT_s[:, 2 + t0:2 + t0 + 512].rearrange("(k p) t -> p k t", p=128), in_=st[:])], reads=[st.b])
                sc.barrier()
            stop('prologue')

            chains = []

            def pump(n=1):
                for _ in range(n):
                    for g in list(chains):
                        try:
                            next(g)
                        except StopIteration:
                            chains.remove(g)

            def drain():
                while chains:
                    pump()


            for l in range(n_layers):
                x_res = x_in if l == 0 else x_s
                last = (l == n_layers - 1)
                x_dst = out_d if last else x_s

                with ExitStack() as es:
                    win = sb(es, "win", [128, 8, NIN], BF16, nb=2)
                    sc.dma("pool", [lambda q, k=k: q.dma_start(out=win[:, k, 768:1280], in_=w_in_d[l, k * 128:(k + 1) * 128, 768:1280]) for k in range(8)],
                           writes=[win.bs[1]])
                    pp = sb(es, "pp", [128, 28])
                    pdv = sb(es, "pdv", [128, 16])
                    pdv.b.strict = True
                    ptmp = sb(es, "ptmp", [128, 12])
                    ptmp.b.strict = True
                    wbd = sb(es, "wbd", [128, 8, 128], BF16)
                    tbp = sb(es, "tbp", [128, 4, W])
                    bsT = sb(es, "bsT", [128, 2, 128])
                    wsr = sb(es, "wsr", [128, 4, 128], BF16)
                    wsT = sb(es, "wsT", [128, 4, 128], BF16)
                    Lall = sb(es, "Lall", [128, NCH, 256], BF16)
                    cosT = sb(es, "cosT", [128, NCH, 32])
                    sinT = sb(es, "sinT", [128, NCH, 32])
                    make_tables(es, cosT, sinT)
                    stop('tables')
                    sc.dma("sp", [lambda q: q.dma_start(out=pp[:], in_=pp_d[l])], writes=[pp.b])
                    sc.dma("pool", [lambda q: q.dma_start(out=wbd[:], in_=wbd_d[l].rearrange("e p m -> p e m"))], writes=[wbd.b])
                    sc.dma("sp", [lambda q: q.dma_start(out=tbp[:].rearrange("p a w -> p (a w)"),
                                                        in_=tb_d[l].partition_broadcast(128))], writes=[tbp.b])
                    sc.dma("sp", [lambda q, pr=pr, hh=hh: q.dma_start(
                        out=bsT[hh * 64:(hh + 1) * 64, pr, :], in_=bs_d[l, 2 * pr + hh].partition_broadcast(64))
                        for pr in range(2) for hh in range(2)], writes=[bsT.b])
                    sc.dma("pool", [lambda q: q.dma_start(out=wsr[:], in_=ws_d[l].rearrange("h p q -> p h q"))], writes=[wsr.b])

                    sc.dma("pool", [lambda q, k=k: q.dma_start(out=win[:, k, 0:768], in_=w_in_d[l, k * 128:(k + 1) * 128, 0:768]) for k in range(8)],
                           writes=[win.b])
                    sc.dma("pool", [lambda q, k=k: q.dma_start(out=win[:, k, 1280:NIN], in_=w_in_d[l, k * 128:(k + 1) * 128, 1280:NIN]) for k in range(8)],
                           writes=[win.b])
                    pz = [ps(es, "pz%d" % i, [128, 512]) for i in range(2)]
                    ptm1 = ps(es, "ptm1", [128, 512])
                    XA = [ps(es, "xa%d" % i, [128, 512]) for i in range(2)]
                    ptrs = [ps(es, "ptrA%d" % i, [128, 8, 128], BF16) for i in range(2)]
                    pmx = ps(es, "pmx", [128, 512])
                    ptm0 = XA[0]
                    pmu = XA[1]
                    ptr = ptrs[0]

                    transposes(ptr, lambda j: wsr[:, j, :], 4, [wsr.b])
                    sc.op("dve", lambda: V.tensor_copy(out=wsT[:], in_=ptr[:, 0:4, :]), reads=[ptr.b], writes=[wsT.b])

                    lam = pp[:, 18:22]
                    sc.op("act", lambda: A.activation(out=ptmp[:, 0:4], in_=lam, func=AF.Exp, scale=-1.0), reads=[pp.b], writes=[ptmp.b])
                    sc.op("dve", lambda: V.tensor_scalar(out=ptmp[:, 4:8], in0=ptmp[:, 0:4], scalar1=1.0, scalar2=None, op0=ALU.add),
                          reads=[ptmp.b], writes=[ptmp.b])
                    sc.op("act", lambda: A.activation(out=ptmp[:, 8:12], in_=ptmp[:, 4:8], func=AF.Ln), reads=[ptmp.b], writes=[ptmp.b])
                    sc.op("dve", lambda: V.tensor_scalar(out=ptmp[:, 4:8], in0=ptmp[:, 4:8], scalar1=-1.0, scalar2=1e-30,
                                                         op0=ALU.add, op1=ALU.max), reads=[ptmp.b], writes=[ptmp.b])
                    sc.op("dve", lambda: V.reciprocal(out=ptmp[:, 4:8], in_=ptmp[:, 4:8]), reads=[ptmp.b], writes=[ptmp.b])
                    sc.op("dve", lambda: V.tensor_tensor(out=ptmp[:, 8:12], in0=ptmp[:, 8:12], in1=ptmp[:, 0:4], op=ALU.mult),
                          reads=[ptmp.b], writes=[ptmp.b])
                    sc.op("dve", lambda: V.scalar_tensor_tensor(out=pdv[:, 0:4], in0=ptmp[:, 8:12], scalar=-8.0, in1=ptmp[:, 4:8],
                                                                op0=ALU.mult, op1=ALU.mult), reads=[ptmp.b], writes=[pdv.b])
                    sc.op("dve", lambda: V.tensor_scalar(out=pdv[:, 4:8], in0=pdv[:, 0:4], scalar1=0.5, scalar2=None, op0=ALU.mult),
                          reads=[pdv.b], writes=[pdv.b])
                    sc.op("dve", lambda: V.tensor_scalar(out=pdv[:, 8:16], in0=pp[:, 10:18], scalar1=0.5, scalar2=None, op0=ALU.mult),
                          reads=[pp.b, pdv.b], writes=[pdv.b])

                    def tok_mm(pt, xt, tcol, col0, ncols, off=0, wbufs=None):
                        def f():
                            for k in range(8):
                                i = PE.matmul(pt[:, off:off + ncols], lhsT=xt[:, k, tcol:tcol + 128],
                                              rhs=win[:, k, col0:col0 + ncols], start=(k == 0), stop=(k == 7))
                            return i
                        sc.op("pe", f, reads=[xt.b] + (wbufs or [win.b, win.bs[1]]), writes=[pt.b])

                    def rotary(src_ap_fn, nh, n, dst, ra, rb):
                        cosb = fap(cosT, (slice(None), n, slice(0, 1)), [[0, nh], [0, 2], [1, 32]])
                        sinb = fap(sinT, (slice(None), n, slice(0, 1)), [[0, nh], [0, 2], [1, 32]])
                        src, srcb, src_sw = src_ap_fn()
                        nn = nh * 64
                        v4 = lambda t: t[:, 0:nn].rearrange("p (h a f) -> p h a f", h=nh, a=2)
                        sc.op("dve", lambda: V.tensor_tensor(out=v4(ra), in0=src, in1=cosb, op=ALU.mult),
                              reads=[srcb, cosT.b], writes=[ra.b])
                        sc.op("dve", lambda: V.tensor_tensor(out=v4(rb), in0=src_sw, in1=sinb, op=ALU.mult),
                              reads=[srcb, sinT.b], writes=[rb.b])
                        sc.op("dve", lambda: V.tensor_tensor(out=v4(dst)[:, :, 0, :], in0=v4(ra)[:, :, 0, :], in1=v4(rb)[:, :, 0, :],
                                                             op=ALU.subtract), reads=[ra.b, rb.b], writes=[dst.b])
                        sc.op("dve", lambda: V.tensor_tensor(out=v4(dst)[:, :, 1, :], in0=v4(ra)[:, :, 1, :], in1=v4(rb)[:, :, 1, :],
                                                             op=ALU.add), reads=[ra.b, rb.b], writes=[dst.b])

                    def ps_views(pt, c0, nh):
                        src = pt[:, c0:c0 + nh * 64].rearrange("p (h a f) -> p h a f", h=nh, a=2)
                        sw = fap(pt, (slice(None), slice(c0 + 32, c0 + 33)), [[64, nh], [-32, 2], [1, 32]])
                        return src, pt.b, sw

                    with ExitStack() as es1:
                        xt = [sb(es1, "xtA%d" % i, [128, 8, TA + 4], BF16) for i in range(2)]
                        ra = sb(es1, "ra", [128, 512])
                        rb = sb(es1, "rb", [128, 512])
                        rot = sb(es1, "rot", [128, 512])
                        kb = sb(es1, "kb", [128, 256], BF16)
                        vbf = sb(es1, "vbf", [128, 256], BF16)
                        Lst = sb(es1, "Lst", [128, 256])

                        def load_xt(m):
                            b = xt[m % 2]
                            t0 = m * TA
                            sc.dma("sp", [lambda q: q.dma_start(
                                out=b[:], in_=xT_s[:, t0:t0 + TA + 4].rearrange("(k p) t -> p k t", p=128))], writes=[b.b])

                        stop('s0_pre')
                        NM = S // TA
                        sc.op("dve", lambda: V.memset(Lst[:], 0.0), writes=[Lst.b])
                        load_xt(NM - 1)
                        for m in reversed(range(NM)):
                            if stage == 's0_1' and m < NM - 1:
                                stop('s0_1')
                            if m > 0:
                                load_xt(m - 1)
                            for s in reversed(range(TA // 128)):
                                n = m * (TA // 128) + s
                                stop('s0_a')
                                tok_mm(ptm0, xt[m % 2], 2 + s * 128, 768, 512, wbufs=[win.bs[1]])
                                stop('s0_b')
                                rotary(lambda: ps_views(ptm0, 0, 4), 4, n, rot, ra, rb)
                                stop('s0_c')
                                kdb = fap(dec, (slice(None), slice(4, 5)), [[1, 4], [0, 64]])
                                sc.op("dve", lambda: V.tensor_tensor(
                                    out=kb[:].rearrange("p (h d) -> p h d", h=4), in0=rot[:, 0:256].rearrange("p (h d) -> p h d", h=4),
                                    in1=kdb, op=ALU.mult), reads=[rot.b, dec.b], writes=[kb.b])
                                stop('s0_c2')
                                sc.op("act", lambda: A.copy(out=vbf[:], in_=ptm0[:, 256:512]), reads=[ptm0.b], writes=[vbf.b])

                                def f():
                                    for h in range(4):
                                        i = PE.matmul(pmu[64:128, h * 64:(h + 1) * 64], lhsT=kb[:, h * 64:(h + 1) * 64],
                                                      rhs=vbf[:, h * 64:(h + 1) * 64], start=True, stop=True)
                                    return i
                                stop('s0_d')
                                sc.op("pe", f, reads=[kb.b, vbf.b], writes=[pmu.b])
                                stop('s0_e')
                                sc.op("act", lambda n=n: A.copy(out=Lall[64:128, n, :], in_=Lst[64:128, :]), reads=[Lst.b], writes=[Lall.b])
                                sc.op("dve", lambda: V.tensor_tensor(out=Lst[64:128, :], in0=Lst[64:128, :], in1=cdec[64:128, :], op=ALU.mult),
                                      reads=[Lst.b, cdec.b], writes=[Lst.b])
                                sc.op("dve", lambda: V.tensor_tensor(out=Lst[64:128, :], in0=Lst[64:128, :], in1=pmu[64:128, 0:256], op=ALU.add),
                                      reads=[Lst.b, pmu.b], writes=[Lst.b])

                        sc.barrier()
                        stop('sweep0')
                        NS = TA // 128
                        cxs = [sb(es1, "cx", [128, TA + 3])] * 2
                        xf2 = [sb(es1, "xf%d" % i, [128, 2, TA]) for i in range(2)]
                        xfb = sb(es1, "xfb", [128, 2, TA], BF16)
                        tgs2 = [[sb(es1, "tgs%d_%d" % (j, i), [128, TA]) for i in range(4)] for j in range(2)]
                        tas2 = [[sb(es1, "tas%d_%d" % (j, i), [128, TA]) for i in range(4)] for j in range(2)]
                        e2 = sb(es1, "e2", [128, TA])
                        af = sb(es1, "af", [128, TA])
                        bf_ = sb(es1, "bf_", [128, TA])
                        carry = sb(es1, "carry", [128, 2])
                        carry.b.strict = True
                        st4 = [sb(es1, "st4_%d" % i, [128, 4, 2, TA]) for i in range(2)]
                        yt = [sb(es1, "yt%d" % i, [128, 6, TA], BF16) for i in range(2)]
                        gu2 = [sb(es1, "gu%d" % i, [128, 2, TA]) for i in range(2)]
                        hd = sb(es1, "hd", [128, TA + 3])
                        pd_ = hd
                        accd = sb(es1, "accd", [128, TA])

                        def per_s(name, shape, dt=F32, strict=False, n=None):
                            ts = [sb(es1, "%s_%d" % (name, i), shape, dt) for i in range(n or NS)]
                            for t_ in ts:
                                t_.b.strict = strict
                            return ts
                        gv = per_s("gv", [128, 256], n=2 * NS)
                        vln = per_s("vln", [128, 256], BF16)
                        stA = per_s("stA", [128, 6], strict=True, n=2 * NS)
                        mvA = per_s("mvA", [128, 2], strict=True, n=2 * NS)
                        stG = per_s("stG", [128, 4, 6], strict=True)
                        mvG = per_s("mvG", [128, 4, 2], strict=True)
                        rsG = per_s("rsG", [128, 4], strict=True)
                        mixt = [sb(es1, "mixt", [128, 2, 128])] * NS
                        ra_ = [ra] * NS
                        rb_ = [rb] * NS
                        rot_ = [rot] + per_s("rot", [128, 512], n=NS - 1)
                        qkbf = per_s("qkbf", [128, 512], BF16)
                        qcat = per_s("qcat", [128, 512], BF16)
                        kf = [kb] + per_s("kf", [128, 256], BF16, n=NS - 1)
                        vbs = per_s("vbs", [128, 256], BF16, n=2 * NS)
                        tgB = [sb(es1, "tgB", [128, 256])] * (2 * NS)
                        wgB = per_s("wgB", [128, 256], n=2 * NS)
                        trT = per_s("trT", [128, 10, 128], BF16)
                        sT = per_s("sT", [128, 512], BF16)
                        osb = per_s("osb", [128, 256])
                        gn1 = per_s("gn1", [128, 256])
                        gn2 = gn1
                        yB = per_s("yB", [128, 256], BF16)
                        Rst = Lst
                        Rbf = [sb(es1, "Rbf%d" % i, [128, 256], BF16) for i in range(3)]

                        sc.op("dve", lambda: V.memset(Rst[:], 0.0), writes=[Rst.b])
                        sc.op("dve", lambda: V.memset(Rbf[0][:], 0.0), writes=[Rbf[0].b])
                        for t_ in trT:
                            sc.op("dve", lambda t_=t_: V.memset(t_[:], 0.0), writes=[t_.b])
                        sc.op("dve", lambda: V.memset(carry[:], 0.0), writes=[carry.b])
                        pzi = [0]

                        def fm(xtb, col0):
                            pt = pz[pzi[0] % 2]
                            pzi[0] += 1

                            def f():
                                for k in range(8):
                                    i = PE.matmul(pt[:, 0:TA + 3], lhsT=win[:, k, col0:col0 + 128], rhs=xtb[:, k, 0:TA + 3],
                                                  start=(k == 0), stop=(k == 7))
                                return i
                            sc.op("pe", f, reads=[xtb.b, win.b], writes=[pt.b])
                            return pt

                        def ret_chain(s, n, xtb, ytm, mp):
                            sb_ = mp * NS + s
                            X = XA[s]
                            ptr = ptrs[s]
                            tc = 2 + s * 128
                            ra, rb, rot = ra_[s], rb_[s], rot_[s]
                            tok_mm(X, xtb, tc, 512, 512)
                            rotary(lambda: ps_views(X, 0, 8), 8, n, rot, ra, rb)
                            yield
                            sc.op("act", lambda: A.copy(out=qkbf[s][:], in_=rot[:]), reads=[rot.b], writes=[qkbf[s].b])
                            qdb = fap(dec, (slice(None), slice(8, 9)), [[2, 4], [1, 2], [0, 64]])
                            sc.op("dve", lambda: V.tensor_tensor(
                                out=qcat[s][:].rearrange("p (h a d) -> p h a d", h=4, a=2),
                                in0=fap(rot, (slice(None), slice(0, 1)), [[64, 4], [0, 2], [1, 64]]), in1=qdb, op=ALU.mult),
                                reads=[rot.b, dec.b], writes=[qcat[s].b])
                            kdf = fap(dec, (slice(None), slice(0, 1)), [[1, 4], [0, 64]])
                            sc.op("dve", lambda: V.tensor_tensor(
                                out=kf[s][:].rearrange("p (h d) -> p h d", h=4), in0=rot[:, 256:512].rearrange("p (h d) -> p h d", h=4),
                                in1=kdf, op=ALU.mult), reads=[rot.b, dec.b], writes=[kf[s].b])
                            yield

                            def f():
                                for j in range(4):
                                    PE.transpose(out=ptr[:, j, :], in_=qkbf[s][:, j * 128:(j + 1) * 128], identity=ident[:])
                                for j in range(4):
                                    i = PE.transpose(out=ptr[:, 4 + j, :], in_=qcat[s][:, j * 128:(j + 1) * 128], identity=ident[:])
                                return i
                            sc.op("pe", f, reads=[qkbf[s].b, qcat[s].b, ident.b], writes=[ptr.b])
                            sc.op("act", lambda: A.copy(out=trT[s][:, 0:6, :], in_=ptr[:, 2:8, :]), reads=[ptr.b], writes=[trT[s].b])
                            sc.op("act", lambda: A.copy(out=fap(trT[s], (slice(0, 64), 6, slice(0, 1)), [[256, 2], [1, 128]]), in_=ptr[0:64, 0:2, :]),
                                  reads=[ptr.b], writes=[trT[s].b])
                            sc.op("act", lambda: A.copy(out=fap(trT[s], (slice(64, 128), 7, slice(0, 1)), [[256, 2], [1, 128]]), in_=ptr[64:128, 0:2, :]),
                                  reads=[ptr.b], writes=[trT[s].b])
                            rc = Rbf[n % 3]
                            sc.op("act", lambda: A.copy(out=rc[64:128, :], in_=Lall[64:128, n, :]), reads=[Lall.b], writes=[rc.b])
                            yield

                            def f():
                                for h in range(4):
                                    i = PE.matmul(X[:, h * 128:(h + 1) * 128], lhsT=trT[s][:, h // 2, :], rhs=trT[s][:, 6 + h, :], start=True, stop=True)
                                return i
                            sc.op("pe", f, reads=[trT[s].b], writes=[X.b])
                            sc.op("dve", lambda: V.tensor_tensor(out=sT[s][:], in0=X[:], in1=maskT[:], op=ALU.mult),
                                  reads=[X.b, maskT.b], writes=[sT[s].b])
                            yield
                            def f():
                                for h in range(4):
                                    i = PE.matmul(X[0:64, h * 64:(h + 1) * 64], lhsT=kf[s][:, h * 64:(h + 1) * 64],
                                                  rhs=vbs[sb_][:, h * 64:(h + 1) * 64], start=True, stop=True)
                                return i
                            sc.op("pe", f, reads=[kf[s].b, vbs[sb_].b], writes=[X.b])
                            sc.op("dve", lambda: V.tensor_tensor(out=Rst[0:64, :], in0=Rst[0:64, :], in1=cdec[0:64, :], op=ALU.mult),
                                  reads=[Rst.b, cdec.b], writes=[Rst.b])
                            sc.op("dve", lambda: V.tensor_tensor(out=Rst[0:64, :], in0=Rst[0:64, :], in1=X[0:64, 0:256], op=ALU.add),
                                  reads=[Rst.b, X.b], writes=[Rst.b])
                            rn = Rbf[(n + 1) % 3]
                            sc.op("act", lambda: A.copy(out=rn[0:64, :], in_=Rst[0:64, :]), reads=[Rst.b], writes=[rn.b])
                            yield
                            def f():
                                for h in range(4):
                                    o = X[:, h * 64:(h + 1) * 64]
                                    PE.matmul(o, lhsT=sT[s][:, h * 128:(h + 1) * 128], rhs=vbs[sb_][:, h * 64:(h + 1) * 64], start=True, stop=False)
                                    i = PE.matmul(o, lhsT=trT[s][:, 2 + h, :], rhs=rc[:, h * 64:(h + 1) * 64], start=False, stop=True)
                                return i
                            sc.op("pe", f, reads=[sT[s].b, vbs[sb_].b, trT[s].b, rc.b], writes=[X.b])
                            sc.op("act", lambda: A.copy(out=osb[s][:], in_=X[:, 0:256]), reads=[X.b], writes=[osb[s].b])
                            yield
                            def f():
                                for h in range(4):
                                    i = V.bn_stats(out=stG[s][:, h, :], in_=osb[s][:, h * 64:(h + 1) * 64])
                                return i
                            sc.op("dve", f, reads=[osb[s].b], writes=[stG[s].b])

                            def f():
                                for h in range(4):
                                    i = V.bn_aggr(out=mvG[s][:, h, :], in_=stG[s][:, h, :])
                                return i
                            sc.op("dve", f, reads=[stG[s].b], writes=[mvG[s].b])
                            yield
                            sc.op("act", lambda: A.activation(out=rsG[s][:], in_=mvG[s][:, :, 1], func=AF.Ln, bias=cst[:, 0:1]),
                                  reads=[mvG[s].b, cst.b], writes=[rsG[s].b])
                            sc.op("act", lambda: A.activation(out=rsG[s][:], in_=rsG[s][:], func=AF.Exp, scale=-0.5), reads=[rsG[s].b], writes=[rsG[s].b])
                            yield
                            g3 = lambda t: t[:].rearrange("p (h d) -> p h d", h=4)
                            sc.op("dve", lambda: V.tensor_tensor(out=g3(gn1[s]), in0=g3(osb[s]),
                                                                 in1=fap(mvG[s], (slice(None), 0, slice(0, 1)), [[2, 4], [0, 64]]), op=ALU.subtract),
                                  reads=[osb[s].b, mvG[s].b], writes=[gn1[s].b])
                            sc.op("dve", lambda: V.tensor_tensor(out=g3(gn1[s]), in0=g3(gn1[s]), in1=fap(rsG[s], (slice(None), slice(0, 1)), [[1, 4], [0, 64]]),
                                                                 op=ALU.mult), reads=[gn1[s].b, rsG[s].b], writes=[gn1[s].b])
                            yield
                            sc.op("dve", lambda: V.tensor_tensor(out=gn1[s][:], in0=gn1[s][:], in1=tbp[:, 2, :], op=ALU.mult), reads=[gn1[s].b, tbp.b], writes=[gn1[s].b])
                            sc.op("dve", lambda: V.tensor_tensor(out=gn2[s][:], in0=gn1[s][:], in1=tbp[:, 3, :], op=ALU.add), reads=[gn1[s].b, tbp.b], writes=[gn2[s].b])
                            yield
                            sc.op("dve", lambda: V.scalar_tensor_tensor(out=yB[s][:], in0=gn2[s][:], scalar=0.5, in1=wgB[sb_][:], op0=ALU.mult, op1=ALU.mult),
                                  reads=[gn2[s].b, wgB[sb_].b], writes=[yB[s].b])
                            yield
                            transposes(ptr, lambda j: yB[s][:, j * 128:(j + 1) * 128], 2, [yB[s].b])
                            sc.op("act", lambda: A.copy(out=ytm[:, 2:4, s * 128:(s + 1) * 128], in_=ptr[:, 0:2, :]), reads=[ptr.b], writes=[ytm.b])
                            yield

                        def gmlp_chain(s, ytm, mp):
                            sb_ = mp * NS + s
                            gu = gu2[mp]
                            sc.op("act", lambda: A.activation(out=mvA[sb_][:, 1:2], in_=mvA[sb_][:, 1:2], func=AF.Ln, bias=cst[:, 0:1]),
                                  reads=[mvA[sb_].b, cst.b], writes=[mvA[sb_].b])
                            sc.op("act", lambda: A.activation(out=mvA[sb_][:, 1:2], in_=mvA[sb_][:, 1:2], func=AF.Exp, scale=-0.5), reads=[mvA[sb_].b], writes=[mvA[sb_].b])
                            yield
                            sc.op("dve", lambda: V.tensor_scalar(out=gv[sb_][:], in0=gv[sb_][:], scalar1=mvA[sb_][:, 0:1], scalar2=mvA[sb_][:, 1:2],
                                                                 op0=ALU.subtract, op1=ALU.mult), reads=[gv[sb_].b, mvA[sb_].b], writes=[gv[sb_].b])
                            yield
                            sc.op("dve", lambda: V.tensor_tensor(out=gv[sb_][:], in0=gv[sb_][:], in1=tbp[:, 0, :], op=ALU.mult), reads=[gv[sb_].b, tbp.b], writes=[gv[sb_].b])
                            sc.op("dve", lambda: V.tensor_tensor(out=vln[s][:], in0=gv[sb_][:], in1=tbp[:, 1, :], op=ALU.add), reads=[gv[sb_].b, tbp.b], writes=[vln[s].b])
                            yield

                            def f():
                                for h in range(4):
                                    i = PE.matmul(pmx[(h % 2) * 64:(h % 2) * 64 + 64, 256 + (h // 2) * 128:256 + (h // 2) * 128 + 128],
                                                  lhsT=vln[s][:, h * 64:(h + 1) * 64], rhs=wsT[:, h, :], start=True, stop=True)
                                return i
                            sc.op("pe", f, reads=[vln[s].b, wsT.b], writes=[pmx.b])
                            sc.op("dve", lambda: V.tensor_tensor(out=mixt[s][:], in0=pmx[:, 256:512].rearrange("p (a q) -> p a q", a=2), in1=bsT[:],
                                                                 op=ALU.add), reads=[pmx.b, bsT.b], writes=[mixt[s].b])
                            sc.op("dve", lambda: V.tensor_tensor(out=ytm[:, 0:2, s * 128:(s + 1) * 128], in0=mixt[s][:],
                                                                 in1=gu[:, :, s * 128:(s + 1) * 128], op=ALU.mult),
                                  reads=[mixt[s].b, gu.b], writes=[ytm.b])
                            yield

                        def lru_chain(s4, t0, mp):
                            xf, tgs, tas = xf2[mp], tgs2[mp], tas2[mp]
                            for c in range(2):
                                for z in range(2):
                                    zc = z * 2 + c
                                    adst = af[:] if z == 0 else s4[:, 2, c, :]
                                    adb = af.b if z == 0 else s4.b
                                    sc.op("act", lambda: A.activation(out=adst, in_=tgs[zc][:], func=AF.Exp, scale=pdv[:, 4 + zc:5 + zc],
                                                                      bias=pdv[:, 4 + zc:5 + zc]), reads=[tgs[zc].b, pdv.b], writes=[adb])
                                    sc.op("act", lambda: A.activation(out=e2[:], in_=tgs[zc][:], func=AF.Exp, scale=pdv[:, zc:zc + 1],
                                                                      bias=pdv[:, zc:zc + 1]), reads=[tgs[zc].b, pdv.b], writes=[e2.b])
                                    sc.op("act", lambda: A.activation(out=e2[:], in_=e2[:], func=AF.Ln, scale=-1.0, bias=cst[:, 1:2]),
                                          reads=[e2.b, cst.b], writes=[e2.b])
                                    sc.op("act", lambda: A.activation(out=e2[:], in_=e2[:], func=AF.Exp, scale=0.5, bias=cst[:, 2:3]),
                                          reads=[e2.b, cst.b], writes=[e2.b])
                                    yield
                                    bdst = bf_[:] if z == 0 else s4[:, 3, c, :]
                                    bdb = bf_.b if z == 0 else s4.b
                                    sc.op("dve", lambda: V.scalar_tensor_tensor(out=tas[zc][:], in0=tas[zc][:], scalar=1.0, in1=xf[:, c, :],
                                                                                op0=ALU.add, op1=ALU.mult), reads=[tas[zc].b, xf.b], writes=[tas[zc].b])
                                    sc.op("dve", lambda: V.tensor_tensor(out=bdst, in0=tas[zc][:], in1=e2[:], op=ALU.mult),
                                          reads=[tas[zc].b, e2.b], writes=[bdb])
                                    yield
                                    if z == 0:
                                        sc.op("dve", lambda: V.tensor_tensor_scan(out=s4[:, 0, c, :], data0=af[:], data1=bf_[:], initial=carry[:, c:c + 1],
                                                                                  op0=ALU.mult, op1=ALU.add), reads=[af.b, bf_.b, carry.b], writes=[s4.b])
                                        sc.op("dve", lambda: V.tensor_copy(out=carry[:, c:c + 1], in_=s4[:, 0, c, TA - 1:TA]), reads=[s4.b], writes=[carry.b])
                                        sc.op("dve", lambda: V.tensor_tensor(out=s4[:, 0, c, :], in0=s4[:, 0, c, :], in1=s4[:, 1, c, :], op=ALU.mult),
                                              reads=[s4.b], writes=[s4.b])
                                        yield
                            sc.dma("sp", [lambda q: q.dma_start(
                                out=lru_s[:, :, t0:t0 + TA].rearrange("a (c p) t -> p a c t", p=128), in_=s4[:])], reads=[s4.b])
                            yield

                        def d_chain(xtb, ytm):
                            for c in range(2):
                                pt = fm(xtb, 2560 + c * 128)
                                sc.op("act", lambda: A.copy(out=hd[:], in_=pt[:, 0:TA + 3]), reads=[pt.b], writes=[hd.b])
                                yield
                                pt = fm(xtb, 2304 + c * 128)
                                sc.op("dve", lambda: V.tensor_tensor(out=pd_[:], in0=pt[:, 0:TA + 3], in1=hd[:], op=ALU.mult),
                                      reads=[pt.b, hd.b], writes=[pd_.b])
                                yield
                                sc.op("act", lambda: A.activation(out=accd[:], in_=pd_[:, 1:1 + TA], func=AF.Identity, scale=pp[:, 22 + c * 3:23 + c * 3]),
                                      reads=[pd_.b, pp.b], writes=[accd.b])
                                yield
                                for k in range(1, 3):
                                    sc.op("dve", lambda k=k: V.scalar_tensor_tensor(
                                        out=accd[:], in0=pd_[:, 1 + k:1 + k + TA], scalar=pp[:, 22 + c * 3 + k:23 + c * 3 + k], in1=accd[:],
                                        op0=ALU.mult, op1=ALU.add), reads=[pd_.b, pp.b, accd.b], writes=[accd.b])
                                yield
                                pt = fm(xtb, 2048 + c * 128)
                                sc.op("dve", lambda: V.tensor_tensor(out=ytm[:, 4 + c, :], in0=pt[:, 2:2 + TA], in1=accd[:], op=ALU.mult),
                                      reads=[pt.b, accd.b], writes=[ytm.b])
                                yield

                        def stage1(m):
                            xtb = xt[m % 2]
                            mp = m % 2
                            s4 = st4[mp]
                            xf, tgs, tas, gu = xf2[mp], tgs2[mp], tas2[mp], gu2[mp]
                            for c in range(2):
                                pt = fm(xtb, 1536 + c * 128)
                                cx = cxs[c]
                                sc.op("act", lambda: A.copy(out=cx[:], in_=pt[:, 0:TA + 3]), reads=[pt.b], writes=[cx.b])
                                sc.op("dve", lambda: V.tensor_scalar(out=xf[:, c, :], in0=cx[:, 0:TA], scalar1=pp[:, c * 4:c * 4 + 1],
                                                                      scalar2=pp[:, 8 + c:9 + c], op0=ALU.mult, op1=ALU.add),
                                      reads=[cx.b, pp.b], writes=[xf.b])
                                for k in range(1, 4):
                                    sc.op("dve", lambda k=k: V.scalar_tensor_tensor(
                                        out=xf[:, c, :], in0=cx[:, k:k + TA], scalar=pp[:, c * 4 + k:c * 4 + k + 1], in1=xf[:, c, :],
                                        op0=ALU.mult, op1=ALU.add), reads=[cx.b, pp.b, xf.b], writes=[xf.b])
                                sc.op("act", lambda: A.copy(out=xfb[:, c, :], in_=xf[:, c, :]), reads=[xf.b], writes=[xfb.b])
                            yield
                            for c in range(2):
                                pt = fm(xtb, 1792 + c * 128)
                                sc.op("act", lambda: A.activation(out=s4[:, 1, c, :], in_=pt[:, 2:2 + TA], func=AF.Gelu_apprx_tanh),
                                      reads=[pt.b], writes=[s4.b])
                                pt = fm(xtb, 0 + c * 128)
                                sc.op("act", lambda: A.activation(out=gu[:, c, :], in_=pt[:, 2:2 + TA], func=AF.Gelu_apprx_tanh),
                                      reads=[pt.b], writes=[gu.b])
                            yield
                            for c in range(2):
                                for z in range(2):
                                    zc = z * 2 + c
                                    for t_, dst, boff in ((0, tgs[zc], 8), (1, tas[zc], 12)):
                                        pt = pz[pzi[0] % 2]
                                        pzi[0] += 1
                                        sc.op("pe", lambda: PE.matmul(pt[:, 0:TA], lhsT=wbd[:, (z * 2 + t_) * 2 + c, :], rhs=xfb[:, c, :],
                                                                      start=True, stop=True), reads=[wbd.b, xfb.b], writes=[pt.b])
                                        sc.op("act", lambda: A.activation(
                                            out=dst[:], in_=pt[:, 0:TA], func=AF.Tanh, scale=0.5, bias=pdv[:, boff + zc:boff + 1 + zc]),
                                            reads=[pt.b, pdv.b], writes=[dst.b])
                            yield
                            for s in range(NS):
                                tc = 2 + s * 128
                                sb_ = mp * NS + s
                                tok_mm(ptm1, xtb, tc, 256, 256)
                                sc.op("act", lambda: A.activation(out=gv[sb_][:], in_=ptm1[:, 0:256], func=AF.Gelu_apprx_tanh), reads=[ptm1.b], writes=[gv[sb_].b])
                                sc.op("dve", lambda: V.bn_stats(out=stA[sb_][:], in_=gv[sb_][:]), reads=[gv[sb_].b], writes=[stA[sb_].b])
                                sc.op("dve", lambda: V.bn_aggr(out=mvA[sb_][:], in_=stA[sb_][:]), reads=[stA[sb_].b], writes=[mvA[sb_].b])
                                tok_mm(ptm1, xtb, tc, 1024, 512)
                                sc.op("act", lambda: A.copy(out=vbs[sb_][:], in_=ptm1[:, 0:256]), reads=[ptm1.b], writes=[vbs[sb_].b])
                                sc.op("act", lambda: A.activation(out=tgB[sb_][:], in_=ptm1[:, 256:512], func=AF.Tanh, scale=0.5), reads=[ptm1.b], writes=[tgB[sb_].b])
                                sc.op("dve", lambda: V.scalar_tensor_tensor(out=wgB[sb_][:], in0=tgB[sb_][:], scalar=1.0, in1=ptm1[:, 256:512],
                                                                            op0=ALU.add, op1=ALU.mult), reads=[tgB[sb_].b, ptm1.b], writes=[wgB[sb_].b])
                            yield

                        load_xt(0)
                        for _ in stage1(0):
                            pass
                        pendingA = []

                        def flushA():
                            for item in list(pendingA):
                                gs, fn = item
                                if not any(g in chains for g in gs):
                                    fn()
                                    pendingA.remove(item)

                        def store_tile(ytm, t0):
                            sc.dma("sp", [
                                lambda q: q.dma_start(out=yT_s[0:512, t0:t0 + TA].rearrange("(k p) t -> p k t", p=128), in_=ytm[:, 0:4, :]),
                                lambda q: q.dma_start(out=yT_s[768:1024, t0:t0 + TA].rearrange("(k p) t -> p k t", p=128), in_=ytm[:, 4:6, :]),
                            ], reads=[ytm.b])

                        for m in range(NM):
                            xtb = xt[m % 2]
                            mp = m % 2
                            t0 = m * TA
                            s4 = st4[mp]
                            ytm = yt[mp]
                            mine = [lru_chain(s4, t0, mp)]
                            for s in range(NS):
                                mine.append(ret_chain(s, m * NS + s, xtb, ytm, mp))
                            for s in range(NS):
                                mine.append(gmlp_chain(s, ytm, mp))
                            mine.append(d_chain(xtb, ytm))
                            chains.extend(mine)
                            pendingA.append((mine, lambda ytm=ytm, t0=t0: store_tile(ytm, t0)))
                            for r in range(1, S1_ROUND + 1):
                                pump()
                                flushA()
                                if r == 4 and m + 1 < NM:
                                    load_xt(m + 1)
                            if m + 1 < NM:
                                for _ in stage1(m + 1):
                                    pass
                        while chains or pendingA:
                            pump()
                            flushA()
                    sc.barrier()
                    stop('phaseA')

                def ln_chain(T, po, resid, lnp, dst_dram, row0, stT, scol, want_T, ptT):
                    s_t, stL, mvL, nmr, xn, xbf = T
                    for hf_ in range(2):
                        sc.op("dve", lambda hf_=hf_: V.scalar_tensor_tensor(
                            out=s_t[:, hf_ * 512:(hf_ + 1) * 512], in0=resid[:, hf_ * 512:(hf_ + 1) * 512], scalar=ALPHA,
                            in1=po[:, hf_ * 512:(hf_ + 1) * 512], op0=ALU.mult, op1=ALU.add), reads=[resid.b, po.b], writes=[s_t.b])
                        yield

                    def f():
                        for hf_ in range(2):
                            i = V.bn_stats(out=stL[:, hf_, :], in_=s_t[:, hf_ * 512:(hf_ + 1) * 512])
                        return i
                    sc.op("dve", f, reads=[s_t.b], writes=[stL.b])
                    yield
                    sc.op("dve", lambda: V.bn_aggr(out=mvL[:], in_=stL[:]), reads=[stL.b], writes=[mvL.b])
                    yield
                    sc.op("act", lambda: A.activation(out=mvL[:, 1:2], in_=mvL[:, 1:2], func=AF.Ln, bias=cst[:, 0:1]), reads=[mvL.b, cst.b], writes=[mvL.b])
                    sc.op("act", lambda: A.activation(out=mvL[:, 1:2], in_=mvL[:, 1:2], func=AF.Exp, scale=-0.5), reads=[mvL.b], writes=[mvL.b])
                    yield
                    sc.op("dve", lambda: V.scalar_tensor_tensor(out=nmr[:], in0=mvL[:, 0:1], scalar=-1.0, in1=mvL[:, 1:2], op0=ALU.mult, op1=ALU.mult),
                          reads=[mvL.b], writes=[nmr.b])
                    yield
                    sc.op("act", lambda: A.activation(out=xn[:], in_=s_t[:], func=AF.Identity, scale=mvL[:, 1:2], bias=nmr[:]),
                          reads=[s_t.b, mvL.b, nmr.b], writes=[xn.b])
                    yield
                    sc.op("dve", lambda: V.tensor_tensor(out=xn[:], in0=xn[:], in1=lnp[:, 0, :], op=ALU.mult), reads=[xn.b, lnp.b], writes=[xn.b])
                    yield
                    sc.op("dve", lambda: V.tensor_tensor(out=xn[:], in0=xn[:], in1=lnp[:, 1, :], op=ALU.add), reads=[xn.b, lnp.b], writes=[xn.b])
                    yield
                    sc.dma("sp", [lambda q: q.dma_start(out=dst_dram[row0:row0 + 128, :], in_=xn[:])], reads=[xn.b])
                    if want_T:
                        sc.op("act", lambda: A.copy(out=xbf[:], in_=xn[:]), reads=[xn.b], writes=[xbf.b])
                        yield
                        transposes(ptT, lambda j: xbf[:, j * 128:(j + 1) * 128], 8, [xbf.b])
                        sc.op("dve", lambda: V.tensor_copy(out=stT[:, :, scol:scol + 128], in_=ptT[:]), reads=[ptT.b], writes=[stT.b])
                    yield

                def ln_tiles(es, tag, n):
                    out = []
                    for i in range(n):
                        s_t = sb(es, "s_t%s%d" % (tag, i), [128, D])
                        stL = sb(es, "stL%s%d" % (tag, i), [128, 2, 6])
                        mvL = sb(es, "mvL%s%d" % (tag, i), [128, 2])
                        nmr = sb(es, "nmr%s%d" % (tag, i), [128, 1])
                        xn = sb(es, "xn%s%d" % (tag, i), [128, D])
                        xbf = sb(es, "xbf%s%d" % (tag, i), [128, D], BF16)
                        for t_ in (stL, mvL, nmr):
                            t_.b.strict = True
                        out.append((s_t, stL, mvL, nmr, xn, xbf))
                    return out

                with ExitStack() as esW:
                    wg = sb(esW, "wg", [128, 8, DFF], BF16)
                    wu = sb(esW, "wu", [128, 8, DFF], BF16)

                    with ExitStack() as es:
                        NB = 3
                        NL = 4
                        wout = sb(es, "wout", [128, 8, D], BF16)
                        sc.dma("pool", [lambda q, k=k: q.dma_start(out=wout[:, k, :], in_=w_out_d[l, k * 128:(k + 1) * 128, :]) for k in range(8)],
                               writes=[wout.b])
                        sc.dma("pool", [lambda q, k=k: q.dma_start(out=wg[:, k, :], in_=wg_d[l, k * 128:(k + 1) * 128, :]) for k in range(8)], writes=[wg.b])
                        sc.dma("pool", [lambda q, k=k: q.dma_start(out=wu[:, k, :], in_=wu_d[l, k * 128:(k + 1) * 128, :]) for k in range(8)], writes=[wu.b])
                        lnp = sb(es, "lnpB", [128, 2, D])
                        sc.dma("sp", [lambda q: q.dma_start(out=lnp[:].rearrange("p a d -> p (a d)"),
                                                            in_=lnp_d[l, 0:2 * D].partition_broadcast(128))], writes=[lnp.b])
                        TBm = 256
                        yt2 = [sb(es, "yt2_%d" % i, [128, 8, TBm], BF16) for i in range(2)]
                        l4 = [sb(es, "l4_%d" % i, [128, 4, 2, TBm]) for i in range(2)]
                        hb = sb(es, "hb", [128, TBm])
                        carryB = sb(es, "carryB", [128, 2])
                        carryB.b.strict = True
                        xr = [sb(es, "xr%d" % i, [128, D]) for i in range(NB)]
                        LT = ln_tiles(es, "B", NL)
                        stT = [sb(es, "stT%d" % i, [128, 8, TBm], BF16) for i in range(2)]
                        po = [ps(es, "poB%d" % i, [128, D]) for i in range(NB)]
                        ptT = [ps(es, "ptTB%d" % i, [128, 8, 128], BF16) for i in range(2)]
                        sc.op("dve", lambda: V.memset(carryB[:], 0.0), writes=[carryB.b])
                        NMB = S // TBm

                        def load_B(m):
                            t0 = m * TBm
                            y = yt2[m % 2]
                            l_ = l4[m % 2]
                            sc.dma("sp", [
                                lambda q: q.dma_start(out=y[:, 0:4, :], in_=yT_s[0:512, t0:t0 + TBm].rearrange("(k p) t -> p k t", p=128)),
                                lambda q: q.dma_start(out=y[:, 6:8, :], in_=yT_s[768:1024, t0:t0 + TBm].rearrange("(k p) t -> p k t", p=128)),
                            ], writes=[y.b])
                            sc.dma("sp", [lambda q: q.dma_start(out=l_[:], in_=lru_s[:, :, t0:t0 + TBm].rearrange("a (c p) t -> p a c t", p=128))],
                                   writes=[l_.b])

                        def rev(ap2):
                            return AP(ap2.tensor, ap2.offset + (ap2.ap[-1][1] - 1) * ap2.ap[-1][0], [list(ap2.ap[0]), [-ap2.ap[-1][0], ap2.ap[-1][1]]])

                        def stT_store(sT_, t0):
                            yield
                            sc.dma("sp", [lambda q: q.dma_start(
                                out=x1T_s[:, t0:t0 + TBm].rearrange("(k p) t -> p k t", p=128), in_=sT_[:])], reads=[sT_.b])
                            yield

                        load_B(NMB - 1)
                        ci = 0
                        pending = []

                        def flush_stores():
                            for item in list(pending):
                                gs, fn = item
                                if not any(g in chains for g in gs):
                                    fn()
                                    pending.remove(item)

                        def pumpB(n=1):
                            for _ in range(n):
                                pump()
                                flush_stores()

                        for m in reversed(range(NMB)):
                            if m > 0:
                                load_B(m - 1)
                            t0 = m * TBm
                            y = yt2[m % 2]
                            l_ = l4[m % 2]
                            sT_ = stT[m % 2]
                            while len(pending) > 1:
                                pumpB()
                            for c in range(2):
                                sc.op("dve", lambda c=c, l_=l_: V.tensor_tensor_scan(out=rev(hb[:]), data0=rev(l_[:, 2, c, :]), data1=rev(l_[:, 3, c, :]),
                                                                                     initial=carryB[:, c:c + 1], op0=ALU.mult, op1=ALU.add),
                                      reads=[l_.b, carryB.b], writes=[hb.b])
                                sc.op("dve", lambda c=c: V.tensor_copy(out=carryB[:, c:c + 1], in_=hb[:, 0:1]), reads=[hb.b], writes=[carryB.b])
                                sc.op("dve", lambda c=c, l_=l_: V.tensor_tensor(out=hb[:], in0=hb[:], in1=l_[:, 1, c, :], op=ALU.mult),
                                      reads=[hb.b, l_.b], writes=[hb.b])
                                sc.op("dve", lambda c=c, l_=l_, y=y: V.tensor_tensor(out=y[:, 4 + c, :], in0=hb[:], in1=l_[:, 0, c, :], op=ALU.add),
                                      reads=[hb.b, l_.b], writes=[y.b])
                            mt = []
                            for s in reversed(range(TBm // 128)):
                                row0 = t0 + s * 128
                                while len(chains) > NL - 1:
                                    pumpB()
                                xr_ = xr[ci % NB]
                                po_ = po[ci % NB]
                                T_ = LT[ci % NL]
                                pt_ = ptT[ci % 2]
                                ci += 1
                                sc.dma("sp", [lambda q, xr_=xr_, row0=row0: q.dma_start(out=xr_[:], in_=x_res[row0:row0 + 128, :])], writes=[xr_.b])

                                def f(y=y, s=s, po_=po_):
                                    for hf_ in range(2):
                                        for k in range(8):
                                            i = PE.matmul(po_[:, hf_ * 512:(hf_ + 1) * 512], lhsT=y[:, k, s * 128:(s + 1) * 128],
                                                          rhs=wout[:, k, hf_ * 512:(hf_ + 1) * 512], start=(k == 0), stop=(k == 7))
                                    return i
                                sc.op("pe", f, reads=[y.b, wout.b], writes=[po_.b])
                                g = ln_chain(T_, po_, xr_, lnp, x1_s, row0, sT_, s * 128, True, pt_)
                                next(g)
                                next(g)
                                chains.append(g)
                                mt.append(g)
                                pumpB(4)
                            pending.append((mt, lambda sT_=sT_, t0=t0: sc.dma("sp", [lambda q: q.dma_start(
                                out=x1T_s[:, t0:t0 + TBm].rearrange("(k p) t -> p k t", p=128), in_=sT_[:])], reads=[sT_.b])))
                        while chains or pending:
                            pumpB()
                        drain()
                        sc.barrier()
                        stop('phaseB')

                    with ExitStack() as es:
                        wd = sb(es, "wd", [128, 22, D], BF16)
                        for k0 in range(0, 22, 6):
                            sc.dma("pool", [lambda q, k=k: q.dma_start(out=wd[:, k, :], in_=wd_d[l, k * 128:(k + 1) * 128, :])
                                            for k in range(k0, min(22, k0 + 6))], writes=[wd.b])
                        lnp = sb(es, "lnpC", [128, 2, D])
                        sc.dma("sp", [lambda q: q.dma_start(out=lnp[:].rearrange("p a d -> p (a d)"),
                                                            in_=lnp_d[l, 2 * D:4 * D].partition_broadcast(128))], writes=[lnp.b])
                        xt1 = [sb(es, "xt1_%d" % i, [128, 8, TC], BF16) for i in range(2)]
                        hh_ = sb(es, "hh", [128, 22, TC], BF16)
                        sg = [sb(es, "sg%d" % i, [128, TC]) for i in range(2)]
                        xr = [sb(es, "xrC%d" % i, [128, D]) for i in range(2)]
                        LT = ln_tiles(es, "C", 2)
                        stT = [sb(es, "stTC%d" % i, [128, 8, TC], BF16) for i in range(2)]
                        pg = [ps(es, "pg%d" % i, [128, 2, TC]) for i in range(3)]
                        po = [ps(es, "poC%d" % i, [128, D]) for i in range(2)]
                        ptT = [ps(es, "ptTC%d" % i, [128, 8, 128], BF16) for i in range(1)]
                        NMC = S // TC

                        def load_C(m):
                            b = xt1[m % 2]
                            t0 = m * TC
                            sc.dma("sp", [lambda q: q.dma_start(out=b[:], in_=x1T_s[:, t0:t0 + TC].rearrange("(k p) t -> p k t", p=128))], writes=[b.b])

                        def store_xT(sT_, t0):
                            sc.dma("sp", [lambda q: q.dma_start(
                                out=xT_s[:, 2 + t0:2 + t0 + TC].rearrange("(k p) t -> p k t", p=128), in_=sT_[:])], reads=[sT_.b])

                        load_C(0)
                        gi = 0
                        pend = None
                        for m in range(NMC):
                            if m + 1 < NMC:
                                load_C(m + 1)
                            t0 = m * TC
                            xb_ = xt1[m % 2]
                            sT_ = stT[m % 2]
                            for fch in range(22):
                                p_ = pg[gi % 3]
                                sg_ = sg[gi % 2]
                                gi += 1

                                def f(p_=p_, fch=fch, xb_=xb_):
                                    for k in range(8):
                                        PE.matmul(p_[:, 0, :], lhsT=wg[:, k, fch * 128:(fch + 1) * 128], rhs=xb_[:, k, :], start=(k == 0), stop=(k == 7))
                                    for k in range(8):
                                        i = PE.matmul(p_[:, 1, :], lhsT=wu[:, k, fch * 128:(fch + 1) * 128], rhs=xb_[:, k, :], start=(k == 0), stop=(k == 7))
                                    return i
                                sc.op("pe", f, reads=[wg.b, wu.b, xb_.b], writes=[p_.b])
                                sc.op("act", lambda p_=p_, sg_=sg_: A.activation(out=sg_[:], in_=p_[:, 0, :], func=AF.Silu), reads=[p_.b], writes=[sg_.b])
                                sc.op("dve", lambda p_=p_, sg_=sg_, fch=fch: V.tensor_tensor(out=hh_[:, fch, :], in0=sg_[:], in1=p_[:, 1, :], op=ALU.mult),
                                      reads=[sg_.b, p_.b], writes=[hh_.b])
                                if fch >= 2:
                                    pump(1)
                            drain()
                            if pend is not None:
                                store_xT(*pend)
                                pend = None
                            for s in range(TC // 128):
                                row0 = t0 + s * 128
                                xr_ = xr[s]
                                po_ = po[s]
                                sc.dma("sp", [lambda q, xr_=xr_, row0=row0: q.dma_start(out=xr_[:], in_=x1_s[row0:row0 + 128, :])], writes=[xr_.b])

                                def f(s=s, po_=po_):
                                    for hf_ in range(2):
                                        for k in range(22):
                                            i = PE.matmul(po_[:, hf_ * 512:(hf_ + 1) * 512], lhsT=hh_[:, k, s * 128:(s + 1) * 128],
                                                          rhs=wd[:, k, hf_ * 512:(hf_ + 1) * 512], start=(k == 0), stop=(k == 21))
                                    return i
                                sc.op("pe", f, reads=[hh_.b, wd.b], writes=[po_.b])
                                chains.append(ln_chain(LT[s], po_, xr_, lnp, x_dst, row0, sT_, s * 128, not last, ptT[0]))
                            if not last:
                                pend = (sT_, t0)
                        drain()
                        if pend is not None:
                            store_xT(*pend)
                        sc.barrier()
        except _Stop:
            pass
        sc.barrier()
    return nc


def _consts():
    h = np.arange(4, dtype=np.float64)
    lg = np.log1p(-np.exp2(-5.0 - h))
    idx = np.arange(128, dtype=np.float64)
    mask = np.exp(lg[None, :, None] * np.abs(idx[:, None, None] - idx[None, None, :])) * 0.125
    dec = np.zeros((128, 16), np.float64)
    dec[:, 0:4] = np.exp(lg[None, :] * (127 - idx)[:, None]) * 0.125
    dec[:, 4:8] = np.exp(lg[None, :] * idx[:, None]) * 0.125
    qf = np.exp(lg[None, :] * (idx + 1.0)[:, None])
    qb = np.exp(lg[None, :] * (128 - idx)[:, None])
    dec[:, 8:16] = np.stack([qf, qb], axis=2).reshape(128, 8)
    cdec = np.repeat(np.exp(lg * 128), 64)[None, :].repeat(128, 0)
    invf = (10000.0 ** (-np.arange(0, 64, 2, dtype=np.float32) / 64)).astype(np.float32)[None, :].repeat(128, 0)
    return {
        "ident": np.eye(128, dtype=np.float32),
        "maskT": mask.reshape(128, 512).astype(np.float32),
        "dec": dec.astype(np.float32),
        "cdec": cdec.astype(np.float32),
        "invf": np.ascontiguousarray(invf),
    }


def _host_layout(inp):
    f = lambda k: np.asarray(inp[k], dtype=np.float32)
    cw, cb = f("lru_conv_w"), f("lru_conv_b")
    ba, bx, lam, sw = f("lru_ba"), f("lru_bx"), f("lru_lambda"), f("sc_conv_w")
    pp = np.zeros((L, 128, 28), np.float32)
    for c in range(2):
        sl = slice(c * 128, (c + 1) * 128)
        for k in range(4):
            pp[:, :, c * 4 + k] = cw[:, k, sl]
        pp[:, :, 8 + c] = cb[:, sl]
        for z in range(2):
            pp[:, :, 10 + z * 2 + c] = ba[:, z, sl]
            pp[:, :, 14 + z * 2 + c] = bx[:, z, sl]
            pp[:, :, 18 + z * 2 + c] = lam[:, z, sl]
        for k in range(3):
            pp[:, :, 22 + c * 3 + k] = sw[:, k, sl]
    wa, wx = f("lru_wa"), f("lru_wx")
    wbd = np.zeros((L, 8, 128, 128), np.float32)
    for z in range(2):
        for t, wsrc in enumerate((wa, wx)):
            for c in range(2):
                e = (z * 2 + t) * 2 + c
                wbd[:, e, 0:64, 0:64] = wsrc[:, z, 2 * c]
                wbd[:, e, 64:128, 64:128] = wsrc[:, z, 2 * c + 1]
    tb = np.stack([f("gmlp_ln_g"), f("gmlp_ln_b"), f("ret_gn_g"), f("ret_gn_b")], axis=1).reshape(L, 4 * W)
    lnp = np.stack([f("ln1_g"), f("ln1_b"), f("ln2_g"), f("ln2_b")], axis=1).reshape(L, 4 * D)
    shared = {
        "w_in": f("w_in"), "w_out": f("w_out"), "ffn_wg": f("ffn_wg"), "ffn_wu": f("ffn_wu"), "ffn_wd": f("ffn_wd"),
        "pp": pp, "wbd": wbd, "tb": np.ascontiguousarray(tb), "gmlp_bs": f("gmlp_bs"), "gmlp_ws": f("gmlp_ws"),
        "lnp": np.ascontiguousarray(lnp),
    }
    shared.update(_consts())
    return shared


def kernel(**inputs):
    x = np.asarray(inputs["x"], dtype=np.float32)
    pos = np.asarray(inputs["positions"], dtype=np.int32)
    shared = _host_layout(inputs)
    nc = build_program()
    in_maps = []
    for b in range(NCORES):
        m = dict(shared)
        m["x"] = np.ascontiguousarray(x[b])
        m["pos"] = np.ascontiguousarray(pos[b].reshape(NCH, 128).T)
        in_maps.append(m)
    res = run_bass_kernel_spmd(nc, in_maps, core_ids=list(range(NCORES)))
    return np.stack([np.asarray(r["out"], dtype=np.float32) for r in res.results], axis=0)
```

```python
import math
from contextlib import ExitStack

import numpy as np
import concourse.bass as bass
import concourse.mybir as mybir
from concourse.ap import AP
from concourse.bass_utils import run_bass_kernel_spmd

F32 = mybir.dt.float32
BF16 = mybir.dt.bfloat16
I32 = mybir.dt.int32
AF = mybir.ActivationFunctionType
ALU = mybir.AluOpType

D = 1024
S = 8192
L = 4
W = 256
NIN = 2816
DFF = 2816
NCH = S // 128
TA = 256
TB = 512
TC = 256
ALPHA = float((2 * L) ** 0.25)
EPS = 1e-5
NCORES = 8
TWO_PI = float(np.float32(2 * math.pi))
C2PI = float(2 * math.pi - float(np.float32(2 * math.pi)))
PI_LO = 3.1415925
S1_ROUND = 9
C_HI = 6.28125
C_LO = 2 * math.pi - 6.28125


class Buf:
    __slots__ = ("name", "w", "r", "excl", "strict")

    def __init__(self, name, excl=False):
        self.strict = False
        self.name = name
        self.w = None
        self.r = {}
        self.excl = excl


class Sched:
    NSLOT = 8

    def __init__(self, nc, es):
        self.nc = nc
        self.eng = {"pe": nc.tensor, "act": nc.scalar, "dve": nc.vector, "pool": nc.gpsimd, "sp": nc.sync}
        self.sem = {}
        self.cnt = {}
        self.semobj = {}
        for e in self.eng:
            s = es.enter_context(nc.semaphore("s_" + e))
            self.sem[e] = e
            self.semobj[e] = s
            self.cnt[e] = 0
        self.slots = {}
        self.slot_i = {}
        for q in ("sp", "pool"):
            self.slots[q] = []
            for i in range(self.NSLOT):
                key = "d_%s%d" % (q, i)
                self.semobj[key] = es.enter_context(nc.semaphore(key))
                self.cnt[key] = 0
                self.slots[q].append(key)
            self.slot_i[q] = 0
        self.known = {e: {} for e in self.eng}
        self.dead = False

    def _wait(self, e, deps, sself=0):
        k = self.known[e]
        if sself > k.get(e, 0):
            self.eng[e].wait_ge(self.semobj[e], sself)
            k[e] = sself
        for s, v in deps.items():
            if s == e and e != "sp":
                continue
            if k.get(s, 0) < v:
                self.eng[e].wait_ge(self.semobj[s], v)
                k[s] = v

    @staticmethod
    def _deps(reads, writes):
        deps = {}

        def add(t):
            if t is not None and deps.get(t[0], 0) < t[1]:
                deps[t[0]] = t[1]
        for b in reads:
            add(b.w)
            if b.excl:
                for s, v in b.r.items():
                    add((s, v))
        for b in writes:
            add(b.w)
            for s, v in b.r.items():
                add((s, v))
        return deps

    @staticmethod
    def _mark(t, reads, writes):
        for b in writes:
            b.w = t
            b.r = {}
        for b in reads:
            if b.excl:
                b.w = t
                b.r = {}
            elif b.r.get(t[0], 0) < t[1]:
                b.r[t[0]] = t[1]

    def op(self, e, fn, reads=(), writes=(), sync_self=False):
        if self.dead:
            return
        sself = 0
        for b in list(reads) + list(writes):
            if b.strict:
                if b.w is not None and b.w[0] == e:
                    sself = max(sself, b.w[1])
                sself = max(sself, b.r.get(e, 0))
        self._wait(e, self._deps(reads, writes), sself)
        if sync_self and self.cnt[e] > 0:
            self.eng[e].wait_ge(self.semobj[e], self.cnt[e])
        ins = fn()
        self.cnt[e] += 1
        ins.then_inc(self.semobj[e], 1)
        self._mark((e, self.cnt[e]), reads, writes)

    def dma(self, q, fns, reads=(), writes=()):
        if self.dead:
            return
        deps = self._deps(reads, writes)
        slot = self.slots[q][self.slot_i[q] % self.NSLOT]
        self.slot_i[q] += 1
        if self.cnt[slot] > 0:
            deps[slot] = max(deps.get(slot, 0), self.cnt[slot])
        self._wait(q, deps)
        for fn in fns:
            fn(self.eng[q]).then_inc(self.semobj[slot], 16)
        self.cnt[slot] += 16 * len(fns)
        self._mark((slot, self.cnt[slot]), reads, writes)

    def barrier(self):
        if self.dead:
            return
        allv = {s: v for s, v in self.cnt.items() if v > 0}
        for e in self.eng:
            k = self.known[e]
            for s, v in allv.items():
                if s == e:
                    continue
                if k.get(s, 0) < v:
                    self.eng[e].wait_ge(self.semobj[s], v)
                    k[s] = v


class Tl:
    def __init__(self, t, name, nb=1, excl=False):
        self.t = t
        self.b = Buf(name, excl)
        self.bs = [self.b] * nb if excl else [self.b] + [Buf(name + str(i)) for i in range(1, nb)]

    def __getitem__(self, k):
        return self.t[k]


def fap(t, sl, dims):
    base = t[sl]
    return AP(base.tensor, base.offset, [list(base.ap[0])] + [list(d) for d in dims])


class _Stop(Exception):
    pass


def build_program(n_layers=L, debug=False, stage=None):
    nc = bass.Bass("TRN2", target_bir_lowering=False)
    dk = "ExternalOutput" if debug else "Internal"

    def din(name, shape, dt=F32):
        return nc.dram_tensor(name, list(shape), dt, kind="ExternalInput").ap()

    x_in = din("x", [S, D])
    pos_in = din("pos", [128, NCH], I32)
    w_in_d = din("w_in", [L, D, NIN])
    w_out_d = din("w_out", [L, D, D])
    wg_d = din("ffn_wg", [L, D, DFF])
    wu_d = din("ffn_wu", [L, D, DFF])
    wd_d = din("ffn_wd", [L, DFF, D])
    pp_d = din("pp", [L, 128, 28])
    wbd_d = din("wbd", [L, 8, 128, 128])
    tb_d = din("tb", [L, 4 * W])
    bs_d = din("gmlp_bs", [L, 4, 128])
    ws_d = din("gmlp_ws", [L, 4, 128, 128])
    lnp_d = din("lnp", [L, 4 * D])
    ident_d = din("ident", [128, 128])
    mask_d = din("maskT", [128, 512])
    dec_d = din("dec", [128, 16])
    cdec_d = din("cdec", [128, 256])
    invf_d = din("invf", [128, 32])
    out_d = nc.dram_tensor("out", [S, D], F32, kind="ExternalOutput").ap()

    xT_s = nc.dram_tensor("xT_s", [D, S + 4], BF16, kind=dk).ap()
    x_s = nc.dram_tensor("x_s", [S, D], F32, kind="Internal").ap()
    yT_s = nc.dram_tensor("yT_s", [D, S], BF16, kind=dk).ap()
    lru_s = nc.dram_tensor("lru_s", [4, W, S], F32, kind=dk).ap()
    x1_s = nc.dram_tensor("x1_s", [S, D], F32, kind=dk).ap()
    x1T_s = nc.dram_tensor("x1T_s", [D, S], BF16, kind=dk).ap()

    V = nc.vector
    A = nc.scalar
    G = nc.gpsimd
    PE = nc.tensor

    with ExitStack() as es0:
        sc = Sched(nc, es0)

        uid = [0]

        def sb(es, name, shape, dt=F32, nb=1):
            uid[0] += 1
            name = "%s_u%d" % (name, uid[0])
            return Tl(es.enter_context(nc.sbuf_tensor(name, list(shape), dt)), name, nb)

        def ps(es, name, shape, dt=F32, nb=1):
            uid[0] += 1
            name = "%s_u%d" % (name, uid[0])
            return Tl(es.enter_context(nc.psum_tensor(name, list(shape), dt)), name, nb, excl=True)

        def stop(name):
            if stage == name:
                sc.barrier()
                sc.dead = True

        try:
            ident = sb(es0, "ident", [128, 128], BF16)
            maskT = sb(es0, "maskT", [128, 512])
            dec = sb(es0, "dec", [128, 16])
            cdec = sb(es0, "cdec", [128, 256])
            cst = sb(es0, "cst", [128, 4])
            sc.op("dve", lambda: V.memset(cst[:, 0:1], EPS), writes=[cst.b])
            sc.op("dve", lambda: V.memset(cst[:, 1:2], 1.0), writes=[cst.b])
            sc.op("dve", lambda: V.memset(cst[:, 2:3], math.log(0.5)), writes=[cst.b])
            sc.dma("pool", [lambda q: q.dma_start(out=ident[:], in_=ident_d)], writes=[ident.b])
            sc.dma("sp", [lambda q: q.dma_start(out=maskT[:], in_=mask_d)], writes=[maskT.b])
            sc.dma("sp", [lambda q: q.dma_start(out=dec[:], in_=dec_d)], writes=[dec.b])
            sc.dma("sp", [lambda q: q.dma_start(out=cdec[:], in_=cdec_d)], writes=[cdec.b])

            def transposes(pt, src_fn, n, reads):
                def f():
                    for j in range(n):
                        i = PE.transpose(out=pt[:, j, :], in_=src_fn(j), identity=ident[:])
                    return i
                sc.op("pe", f, reads=list(reads) + [ident.b], writes=[pt.b])

            def make_tables(es, cosT, sinT):
                with ExitStack() as est:
                    invf = sb(est, "invf", [128, 32])
                    posi = sb(est, "posi", [128, NCH], I32)
                    posf = sb(est, "posf", [128, NCH])
                    ang = sb(est, "ang", [128, NCH, 32])
                    t1 = sb(est, "t1", [128, NCH, 32])
                    t2 = sb(est, "t2", [128, NCH, 32])
                    sc.dma("sp", [lambda q: q.dma_start(out=invf[:], in_=invf_d)], writes=[invf.b])
                    sc.dma("sp", [lambda q: q.dma_start(out=posi[:], in_=pos_in)], writes=[posi.b])
                    sc.op("dve", lambda: V.tensor_copy(out=posf[:], in_=posi[:]), reads=[posi.b], writes=[posf.b])
                    sc.op("dve", lambda: V.tensor_tensor(
                        out=ang[:], in0=posf[:].unsqueeze(2).to_broadcast([128, NCH, 32]),
                        in1=invf[:].unsqueeze(1).to_broadcast([128, NCH, 32]), op=ALU.mult),
                        reads=[posf.b, invf.b], writes=[ang.b])
                    ki = sb(est, "ki", [128, NCH, 32], I32)
                    sc.op("dve", lambda: V.tensor_scalar(out=t2[:], in0=ang[:], scalar1=1.0 / (2 * math.pi), scalar2=None, op0=ALU.mult),
                          reads=[ang.b], writes=[t2.b])
                    sc.op("dve", lambda: V.tensor_copy(out=ki[:], in_=t2[:]), reads=[t2.b], writes=[ki.b])
                    sc.op("dve", lambda: V.tensor_copy(out=t2[:], in_=ki[:]), reads=[ki.b], writes=[t2.b])
                    sc.op("dve", lambda: V.scalar_tensor_tensor(out=t1[:], in0=t2[:], scalar=-C_HI, in1=ang[:],
                                                                op0=ALU.mult, op1=ALU.add), reads=[t2.b, ang.b], writes=[t1.b])
                    sc.op("dve", lambda: V.scalar_tensor_tensor(out=t1[:], in0=t2[:], scalar=-C_LO, in1=t1[:],
                                                                op0=ALU.mult, op1=ALU.add), reads=[t2.b, t1.b], writes=[t1.b])
                    sc.op("dve", lambda: V.tensor_scalar(out=t2[:], in0=t1[:], scalar1=0.0, scalar2=None, op0=ALU.is_lt),
                          reads=[t1.b], writes=[t2.b])
                    sc.op("dve", lambda: V.scalar_tensor_tensor(out=t1[:], in0=t2[:], scalar=2 * math.pi, in1=t1[:],
                                                                op0=ALU.mult, op1=ALU.add), reads=[t2.b, t1.b], writes=[t1.b])

                    def sin_of(dst, src):
                        sc.op("dve", lambda: V.tensor_scalar(out=t2[:], in0=src[:], scalar1=-math.pi, scalar2=PI_LO,
                                                             op0=ALU.add, op1=ALU.min), reads=[src.b], writes=[t2.b])
                        sc.op("dve", lambda: V.tensor_scalar(out=t2[:], in0=t2[:], scalar1=-PI_LO, scalar2=None, op0=ALU.max),
                              reads=[t2.b], writes=[t2.b])
                        sc.op("act", lambda: A.activation(out=dst[:], in_=t2[:], func=AF.Sin), reads=[t2.b], writes=[dst.b])
                        sc.op("dve", lambda: V.tensor_scalar(out=dst[:], in0=dst[:], scalar1=-1.0, scalar2=None, op0=ALU.mult),
                              reads=[dst.b], writes=[dst.b])

                    sin_of(sinT, t1)
                    sc.op("dve", lambda: V.tensor_scalar(out=t1[:], in0=t1[:], scalar1=math.pi / 2, scalar2=None, op0=ALU.add),
                          reads=[t1.b], writes=[t1.b])
                    sc.op("dve", lambda: V.tensor_scalar(out=ang[:], in0=t1[:], scalar1=2 * math.pi, scalar2=None, op0=ALU.is_ge),
                          reads=[t1.b], writes=[ang.b])
                    sc.op("dve", lambda: V.scalar_tensor_tensor(out=t1[:], in0=ang[:], scalar=-2 * math.pi, in1=t1[:],
                                                                op0=ALU.mult, op1=ALU.add), reads=[ang.b, t1.b], writes=[t1.b])
                    sin_of(cosT, t1)
                    sc.barrier()

            with ExitStack() as es:
                zpad = sb(es, "zpad", [128, 8, 2], BF16)
                sc.op("dve", lambda: V.memset(zpad[:], 0.0), writes=[zpad.b])
                for c0 in (0, S + 2):
                    sc.dma("sp", [lambda q, c0=c0: q.dma_start(
                        out=xT_s[:, c0:c0 + 2].rearrange("(k p) c -> p k c", p=128), in_=zpad[:])], reads=[zpad.b])

                xb = [sb(es, "xb%d" % i, [128, 4, D], BF16) for i in range(2)]
                xst = [sb(es, "xst%d" % i, [128, 8, 512], BF16) for i in range(2)]
                ptr = [ps(es, "ptr%d" % i, [128, 8, 128], BF16) for i in range(2)]
                for m in range(S // 512):
                    t0 = m * 512
                    b = xb[m % 2]
                    st = xst[m % 2]
                    sc.dma("pool", [lambda q, b=b, t0=t0: q.dma_start(
                        out=b[:], in_=x_in[t0:t0 + 512, :].rearrange("(s p) d -> p s d", p=128))], writes=[b.b])
                    for s in range(4):
                        pt = ptr[s % 2]
                        transposes(pt, lambda j, b=b, s=s: b[:, s, j * 128:(j + 1) * 128], 8, [b.b])
                        if s % 2 == 0:
                            sc.op("dve", lambda pt=pt, st=st, s=s: V.tensor_copy(out=st[:, :, s * 128:(s + 1) * 128], in_=pt[:]),
                                  reads=[pt.b], writes=[st.b])
                        else:
                            sc.op("act", lambda pt=pt, st=st, s=s: A.copy(out=st[:, :, s * 128:(s + 1) * 128], in_=pt[:]),
                                  reads=[pt.b], writes=[st.b])
                    sc.dma("sp", [lambda q, st=st, t0=t0: q.dma_start(
                        out=xT_s[:, 2 + t0:2 + t0 + 512].rearrange("(k p) t -> p k t", p=128), in_=st[:])], reads=[st.b])
                sc.barrier()
            stop('prologue')

            chains = []

            def pump(n=1):
                for _ in range(n):
                    for g in list(chains):
                        try:
                            next(g)
                        except StopIteration:
                            chains.remove(g)

            def drain():
                while chains:
                    pump()


            for l in range(n_layers):
                x_res = x_in if l == 0 else x_s
                last = (l == n_layers - 1)
                x_dst = out_d if last else x_s

                with ExitStack() as es:
                    win = sb(es, "win", [128, 8, NIN], BF16, nb=2)
                    sc.dma("pool", [lambda q, k=k: q.dma_start(out=win[:, k, 768:1280], in_=w_in_d[l, k * 128:(k + 1) * 128, 768:1280]) for k in range(8)],
                           writes=[win.bs[1]])
                    pp = sb(es, "pp", [128, 28])
                    pdv = sb(es, "pdv", [128, 16])
                    pdv.b.strict = True
                    ptmp = sb(es, "ptmp", [128, 12])
                    ptmp.b.strict = True
                    wbd = sb(es, "wbd", [128, 8, 128], BF16)
                    tbp = sb(es, "tbp", [128, 4, W])
                    bsT = sb(es, "bsT", [128, 2, 128])
                    wsr = sb(es, "wsr", [128, 4, 128], BF16)
                    wsT = sb(es, "wsT", [128, 4, 128], BF16)
                    Lall = sb(es, "Lall", [128, NCH, 256], BF16)
                    cosT = sb(es, "cosT", [128, NCH, 32])
                    sinT = sb(es, "sinT", [128, NCH, 32])
                    make_tables(es, cosT, sinT)
                    stop('tables')
                    sc.dma("sp", [lambda q: q.dma_start(out=pp[:], in_=pp_d[l])], writes=[pp.b])
                    sc.dma("pool", [lambda q: q.dma_start(out=wbd[:], in_=wbd_d[l].rearrange("e p m -> p e m"))], writes=[wbd.b])
                    sc.dma("sp", [lambda q: q.dma_start(out=tbp[:].rearrange("p a w -> p (a w)"),
                                                        in_=tb_d[l].partition_broadcast(128))], writes=[tbp.b])
                    sc.dma("sp", [lambda q, pr=pr, hh=hh: q.dma_start(
                        out=bsT[hh * 64:(hh + 1) * 64, pr, :], in_=bs_d[l, 2 * pr + hh].partition_broadcast(64))
                        for pr in range(2) for hh in range(2)], writes=[bsT.b])
                    sc.dma("pool", [lambda q: q.dma_start(out=wsr[:], in_=ws_d[l].rearrange("h p q -> p h q"))], writes=[wsr.b])

                    sc.dma("pool", [lambda q, k=k: q.dma_start(out=win[:, k, 0:768], in_=w_in_d[l, k * 128:(k + 1) * 128, 0:768]) for k in range(8)],
                           writes=[win.b])
                    sc.dma("pool", [lambda q, k=k: q.dma_start(out=win[:, k, 1280:NIN], in_=w_in_d[l, k * 128:(k + 1) * 128, 1280:NIN]) for k in range(8)],
                           writes=[win.b])
                    pz = [ps(es, "pz%d" % i, [128, 512]) for i in range(2)]
                    ptm1 = ps(es, "ptm1", [128, 512])
                    XA = [ps(es, "xa%d" % i, [128, 512]) for i in range(2)]
                    ptrs = [ps(es, "ptrA%d" % i, [128, 8, 128], BF16) for i in range(2)]
                    pmx = ps(es, "pmx", [128, 512])
                    ptm0 = XA[0]
                    pmu = XA[1]
                    ptr = ptrs[0]

                    transposes(ptr, lambda j: wsr[:, j, :], 4, [wsr.b])
                    sc.op("dve", lambda: V.tensor_copy(out=wsT[:], in_=ptr[:, 0:4, :]), reads=[ptr.b], writes=[wsT.b])

                    lam = pp[:, 18:22]
                    sc.op("act", lambda: A.activation(out=ptmp[:, 0:4], in_=lam, func=AF.Exp, scale=-1.0), reads=[pp.b], writes=[ptmp.b])
                    sc.op("dve", lambda: V.tensor_scalar(out=ptmp[:, 4:8], in0=ptmp[:, 0:4], scalar1=1.0, scalar2=None, op0=ALU.add),
                          reads=[ptmp.b], writes=[ptmp.b])
                    sc.op("act", lambda: A.activation(out=ptmp[:, 8:12], in_=ptmp[:, 4:8], func=AF.Ln), reads=[ptmp.b], writes=[ptmp.b])
                    sc.op("dve", lambda: V.tensor_scalar(out=ptmp[:, 4:8], in0=ptmp[:, 4:8], scalar1=-1.0, scalar2=1e-30,
                                                         op0=ALU.add, op1=ALU.max), reads=[ptmp.b], writes=[ptmp.b])
                    sc.op("dve", lambda: V.reciprocal(out=ptmp[:, 4:8], in_=ptmp[:, 4:8]), reads=[ptmp.b], writes=[ptmp.b])
                    sc.op("dve", lambda: V.tensor_tensor(out=ptmp[:, 8:12], in0=ptmp[:, 8:12], in1=ptmp[:, 0:4], op=ALU.mult),
                          reads=[ptmp.b], writes=[ptmp.b])
                    sc.op("dve", lambda: V.scalar_tensor_tensor(out=pdv[:, 0:4], in0=ptmp[:, 8:12], scalar=-8.0, in1=ptmp[:, 4:8],
                                                                op0=ALU.mult, op1=ALU.mult), reads=[ptmp.b], writes=[pdv.b])
                    sc.op("dve", lambda: V.tensor_scalar(out=pdv[:, 4:8], in0=pdv[:, 0:4], scalar1=0.5, scalar2=None, op0=ALU.mult),
                          reads=[pdv.b], writes=[pdv.b])
                    sc.op("dve", lambda: V.tensor_scalar(out=pdv[:, 8:16], in0=pp[:, 10:18], scalar1=0.5, scalar2=None, op0=ALU.mult),
                          reads=[pp.b, pdv.b], writes=[pdv.b])

                    def tok_mm(pt, xt, tcol, col0, ncols, off=0, wbufs=None):
                        def f():
                            for k in range(8):
                                i = PE.matmul(pt[:, off:off + ncols], lhsT=xt[:, k, tcol:tcol + 128],
                                              rhs=win[:, k, col0:col0 + ncols], start=(k == 0), stop=(k == 7))
                            return i
                        sc.op("pe", f, reads=[xt.b] + (wbufs or [win.b, win.bs[1]]), writes=[pt.b])

                    def rotary(src_ap_fn, nh, n, dst, ra, rb):
                        cosb = fap(cosT, (slice(None), n, slice(0, 1)), [[0, nh], [0, 2], [1, 32]])
                        sinb = fap(sinT, (slice(None), n, slice(0, 1)), [[0, nh], [0, 2], [1, 32]])
                        src, srcb, src_sw = src_ap_fn()
                        nn = nh * 64
                        v4 = lambda t: t[:, 0:nn].rearrange("p (h a f) -> p h a f", h=nh, a=2)
                        sc.op("dve", lambda: V.tensor_tensor(out=v4(ra), in0=src, in1=cosb, op=ALU.mult),
                              reads=[srcb, cosT.b], writes=[ra.b])
                        sc.op("dve", lambda: V.tensor_tensor(out=v4(rb), in0=src_sw, in1=sinb, op=ALU.mult),
                              reads=[srcb, sinT.b], writes=[rb.b])
                        sc.op("dve", lambda: V.tensor_tensor(out=v4(dst)[:, :, 0, :], in0=v4(ra)[:, :, 0, :], in1=v4(rb)[:, :, 0, :],
                                                             op=ALU.subtract), reads=[ra.b, rb.b], writes=[dst.b])
                        sc.op("dve", lambda: V.tensor_tensor(out=v4(dst)[:, :, 1, :], in0=v4(ra)[:, :, 1, :], in1=v4(rb)[:, :, 1, :],
                                                             op=ALU.add), reads=[ra.b, rb.b], writes=[dst.b])

                    def ps_views(pt, c0, nh):
                        src = pt[:, c0:c0 + nh * 64].rearrange("p (h a f) -> p h a f", h=nh, a=2)
                        sw = fap(pt, (slice(None), slice(c0 + 32, c0 + 33)), [[64, nh], [-32, 2], [1, 32]])
                        return src, pt.b, sw

                    with ExitStack() as es1:
                        xt = [sb(es1, "xtA%d" % i, [128, 8, TA + 4], BF16) for i in range(2)]
                        ra = sb(es1, "ra", [128, 512])
                        rb = sb(es1, "rb", [128, 512])
                        rot = sb(es1, "rot", [128, 512])
                        kb = sb(es1, "kb", [128, 256], BF16)
                        vbf = sb(es1, "vbf", [128, 256], BF16)
                        Lst = sb(es1, "Lst", [128, 256])

                        def load_xt(m):
                            b = xt[m % 2]
                            t0 = m * TA
                            sc.dma("sp", [lambda q: q.dma_start(
                                out=b[:], in_=xT_s[:, t0:t0 + TA + 4].rearrange("(k p) t -> p k t", p=128))], writes=[b.b])

                        stop('s0_pre')
                        NM = S // TA
                        sc.op("dve", lambda: V.memset(Lst[:], 0.0), writes=[Lst.b])
                        with ExitStack() as es0s:
                            rot0 = [rot] + [sb(es0s, "rot0_%d" % i, [128, 512]) for i in range(2)]
                            kb0 = [kb] + [sb(es0s, "kb0_%d" % i, [128, 256], BF16) for i in range(2)]
                            vbf0 = [vbf] + [sb(es0s, "vbf0_%d" % i, [128, 256], BF16) for i in range(2)]
                            X0 = [XA[0], XA[1], pmx]
                            U0 = [pz[0], pz[1], ptm1]

                            def s0_chain(n, xtb, tcol):
                                k = n % 3
                                X, U = X0[k], U0[k]
                                tok_mm(X, xtb, tcol, 768, 512, wbufs=[win.bs[1]])
                                rotary(lambda: ps_views(X, 0, 4), 4, n, rot0[k], ra, rb)
                                sc.op("act", lambda: A.copy(out=vbf0[k][:], in_=X[:, 256:512]), reads=[X.b], writes=[vbf0[k].b])
                                yield
                                kdb = fap(dec, (slice(None), slice(4, 5)), [[1, 4], [0, 64]])
                                sc.op("dve", lambda: V.tensor_tensor(
                                    out=kb0[k][:].rearrange("p (h d) -> p h d", h=4), in0=rot0[k][:, 0:256].rearrange("p (h d) -> p h d", h=4),
                                    in1=kdb, op=ALU.mult), reads=[rot0[k].b, dec.b], writes=[kb0[k].b])
                                yield

                                def f():
                                    for h in range(4):
                                        i = PE.matmul(U[64:128, h * 64:(h + 1) * 64], lhsT=kb0[k][:, h * 64:(h + 1) * 64],
                                                      rhs=vbf0[k][:, h * 64:(h + 1) * 64], start=True, stop=True)
                                    return i
                                sc.op("pe", f, reads=[kb0[k].b, vbf0[k].b], writes=[U.b])
                                sc.op("act", lambda: A.copy(out=Lall[64:128, n, :], in_=Lst[64:128, :]), reads=[Lst.b], writes=[Lall.b])
                                sc.op("dve", lambda: V.tensor_tensor(out=Lst[64:128, :], in0=Lst[64:128, :], in1=cdec[64:128, :], op=ALU.mult),
                                      reads=[Lst.b, cdec.b], writes=[Lst.b])
                                sc.op("dve", lambda: V.tensor_tensor(out=Lst[64:128, :], in0=Lst[64:128, :], in1=U[64:128, 0:256], op=ALU.add),
                                      reads=[Lst.b, U.b], writes=[Lst.b])
                                yield

                            load_xt(NM - 1)
                            for m in reversed(range(NM)):
                                if m > 0:
                                    load_xt(m - 1)
                                for s in reversed(range(TA // 128)):
                                    n = m * (TA // 128) + s
                                    chains.append(s0_chain(n, xt[m % 2], 2 + s * 128))
                                    pump()
                            drain()
                            sc.barrier()
                        sc.barrier()
                        stop('sweep0')
                        NS = TA // 128
                        cxs = [sb(es1, "cx", [128, TA + 3])] * 2
                        xf2 = [sb(es1, "xf%d" % i, [128, 2, TA]) for i in range(2)]
                        xfb = sb(es1, "xfb", [128, 2, TA], BF16)
                        tgs2 = [[sb(es1, "tgs%d_%d" % (j, i), [128, TA]) for i in range(4)] for j in range(2)]
                        tas2 = [[sb(es1, "tas%d_%d" % (j, i), [128, TA]) for i in range(4)] for j in range(2)]
                        e2 = sb(es1, "e2", [128, TA])
                        af = sb(es1, "af", [128, TA])
                        bf_ = sb(es1, "bf_", [128, TA])
                        carry = sb(es1, "carry", [128, 2])
                        carry.b.strict = True
                        st4 = [sb(es1, "st4_%d" % i, [128, 4, 2, TA]) for i in range(2)]
                        yt = [sb(es1, "yt%d" % i, [128, 6, TA], BF16) for i in range(2)]
                        gu2 = [sb(es1, "gu%d" % i, [128, 2, TA]) for i in range(2)]
                        hd = sb(es1, "hd", [128, TA + 3])
                        pd_ = hd
                        accd = sb(es1, "accd", [128, TA])

                        def per_s(name, shape, dt=F32, strict=False, n=None):
                            ts = [sb(es1, "%s_%d" % (name, i), shape, dt) for i in range(n or NS)]
                            for t_ in ts:
                                t_.b.strict = strict
                            return ts
                        gv = per_s("gv", [128, 256], n=2 * NS)
                        vln = per_s("vln", [128, 256], BF16)
                        stA = per_s("stA", [128, 6], strict=True, n=2 * NS)
                        mvA = per_s("mvA", [128, 2], strict=True, n=2 * NS)
                        stG = per_s("stG", [128, 4, 6], strict=True)
                        mvG = per_s("mvG", [128, 4, 2], strict=True)
                        rsG = per_s("rsG", [128, 4], strict=True)
                        mixt = [sb(es1, "mixt", [128, 2, 128])] * NS
                        ra_ = [ra] * NS
                        rb_ = [rb] * NS
                        rot_ = [rot] + per_s("rot", [128, 512], n=NS - 1)
                        qkbf = per_s("qkbf", [128, 512], BF16)
                        qcat = per_s("qcat", [128, 512], BF16)
                        kf = [kb] + per_s("kf", [128, 256], BF16, n=NS - 1)
                        vbs = per_s("vbs", [128, 256], BF16, n=2 * NS)
                        tgB = [sb(es1, "tgB", [128, 256])] * (2 * NS)
                        wgB = per_s("wgB", [128, 256], n=2 * NS)
                        trT = per_s("trT", [128, 10, 128], BF16)
                        sT = per_s("sT", [128, 512], BF16)
                        osb = per_s("osb", [128, 256])
                        gn1 = per_s("gn1", [128, 256])
                        gn2 = gn1
                        yB = per_s("yB", [128, 256], BF16)
                        Rst = Lst
                        Rbf = [sb(es1, "Rbf%d" % i, [128, 256], BF16) for i in range(3)]

                        sc.op("dve", lambda: V.memset(Rst[:], 0.0), writes=[Rst.b])
                        sc.op("dve", lambda: V.memset(Rbf[0][:], 0.0), writes=[Rbf[0].b])
                        for t_ in trT:
                            sc.op("dve", lambda t_=t_: V.memset(t_[:], 0.0), writes=[t_.b])
                        sc.op("dve", lambda: V.memset(carry[:], 0.0), writes=[carry.b])
                        pzi = [0]

                        def fm(xtb, col0):
                            pt = pz[pzi[0] % 2]
                            pzi[0] += 1

                            def f():
                                for k in range(8):
                                    i = PE.matmul(pt[:, 0:TA + 3], lhsT=win[:, k, col0:col0 + 128], rhs=xtb[:, k, 0:TA + 3],
                                                  start=(k == 0), stop=(k == 7))
                                return i
                            sc.op("pe", f, reads=[xtb.b, win.b], writes=[pt.b])
                            return pt

                        def ret_chain(s, n, xtb, ytm, mp):
                            sb_ = mp * NS + s
                            X = XA[s]
                            ptr = ptrs[s]
                            tc = 2 + s * 128
                            ra, rb, rot = ra_[s], rb_[s], rot_[s]
                            tok_mm(X, xtb, tc, 512, 512)
                            rotary(lambda: ps_views(X, 0, 8), 8, n, rot, ra, rb)
                            yield
                            sc.op("act", lambda: A.copy(out=qkbf[s][:], in_=rot[:]), reads=[rot.b], writes=[qkbf[s].b])
                            qdb = fap(dec, (slice(None), slice(8, 9)), [[2, 4], [1, 2], [0, 64]])
                            sc.op("dve", lambda: V.tensor_tensor(
                                out=qcat[s][:].rearrange("p (h a d) -> p h a d", h=4, a=2),
                                in0=fap(rot, (slice(None), slice(0, 1)), [[64, 4], [0, 2], [1, 64]]), in1=qdb, op=ALU.mult),
                                reads=[rot.b, dec.b], writes=[qcat[s].b])
                            kdf = fap(dec, (slice(None), slice(0, 1)), [[1, 4], [0, 64]])
                            sc.op("dve", lambda: V.tensor_tensor(
                                out=kf[s][:].rearrange("p (h d) -> p h d", h=4), in0=rot[:, 256:512].rearrange("p (h d) -> p h d", h=4),
                                in1=kdf, op=ALU.mult), reads=[rot.b, dec.b], writes=[kf[s].b])
                            yield

                            def f():
                                for j in range(4):
                                    PE.transpose(out=ptr[:, j, :], in_=qkbf[s][:, j * 128:(j + 1) * 128], identity=ident[:])
                                for j in range(4):
                                    i = PE.transpose(out=ptr[:, 4 + j, :], in_=qcat[s][:, j * 128:(j + 1) * 128], identity=ident[:])
                                return i
                            sc.op("pe", f, reads=[qkbf[s].b, qcat[s].b, ident.b], writes=[ptr.b])
                            sc.op("act", lambda: A.copy(out=trT[s][:, 0:6, :], in_=ptr[:, 2:8, :]), reads=[ptr.b], writes=[trT[s].b])
                            sc.op("act", lambda: A.copy(out=fap(trT[s], (slice(0, 64), 6, slice(0, 1)), [[256, 2], [1, 128]]), in_=ptr[0:64, 0:2, :]),
                                  reads=[ptr.b], writes=[trT[s].b])
                            sc.op("act", lambda: A.copy(out=fap(trT[s], (slice(64, 128), 7, slice(0, 1)), [[256, 2], [1, 128]]), in_=ptr[64:128, 0:2, :]),
                                  reads=[ptr.b], writes=[trT[s].b])
                            rc = Rbf[n % 3]
                            sc.op("act", lambda: A.copy(out=rc[64:128, :], in_=Lall[64:128, n, :]), reads=[Lall.b], writes=[rc.b])
                            yield

                            def f():
                                for h in range(4):
                                    i = PE.matmul(X[:, h * 128:(h + 1) * 128], lhsT=trT[s][:, h // 2, :], rhs=trT[s][:, 6 + h, :], start=True, stop=True)
                                return i
                            sc.op("pe", f, reads=[trT[s].b], writes=[X.b])
                            sc.op("dve", lambda: V.tensor_tensor(out=sT[s][:], in0=X[:], in1=maskT[:], op=ALU.mult),
                                  reads=[X.b, maskT.b], writes=[sT[s].b])
                            yield
                            def f():
                                for h in range(4):
                                    i = PE.matmul(X[0:64, h * 64:(h + 1) * 64], lhsT=kf[s][:, h * 64:(h + 1) * 64],
                                                  rhs=vbs[sb_][:, h * 64:(h + 1) * 64], start=True, stop=True)
                                return i
                            sc.op("pe", f, reads=[kf[s].b, vbs[sb_].b], writes=[X.b])
                            sc.op("dve", lambda: V.tensor_tensor(out=Rst[0:64, :], in0=Rst[0:64, :], in1=cdec[0:64, :], op=ALU.mult),
                                  reads=[Rst.b, cdec.b], writes=[Rst.b])
                            sc.op("dve", lambda: V.tensor_tensor(out=Rst[0:64, :], in0=Rst[0:64, :], in1=X[0:64, 0:256], op=ALU.add),
                                  reads=[Rst.b, X.b], writes=[Rst.b])
                            rn = Rbf[(n + 1) % 3]
                            sc.op("act", lambda: A.copy(out=rn[0:64, :], in_=Rst[0:64, :]), reads=[Rst.b], writes=[rn.b])
                            yield
                            def f():
                                for h in range(4):
                                    o = X[:, h * 64:(h + 1) * 64]
                                    PE.matmul(o, lhsT=sT[s][:, h * 128:(h + 1) * 128], rhs=vbs[sb_][:, h * 64:(h + 1) * 64], start=True, stop=False)
                                    i = PE.matmul(o, lhsT=trT[s][:, 2 + h, :], rhs=rc[:, h * 64:(h + 1) * 64], start=False, stop=True)
                                return i
                            sc.op("pe", f, reads=[sT[s].b, vbs[sb_].b, trT[s].b, rc.b], writes=[X.b])
                            sc.op("act", lambda: A.copy(out=osb[s][:], in_=X[:, 0:256]), reads=[X.b], writes=[osb[s].b])
                            yield
                            def f():
                                for h in range(4):
                                    i = V.bn_stats(out=stG[s][:, h, :], in_=osb[s][:, h * 64:(h + 1) * 64])
                                return i
                            sc.op("dve", f, reads=[osb[s].b], writes=[stG[s].b])

                            def f():
                                for h in range(4):
                                    i = V.bn_aggr(out=mvG[s][:, h, :], in_=stG[s][:, h, :])
                                return i
                            sc.op("dve", f, reads=[stG[s].b], writes=[mvG[s].b])
                            yield
                            sc.op("act", lambda: A.activation(out=rsG[s][:], in_=mvG[s][:, :, 1], func=AF.Ln, bias=cst[:, 0:1]),
                                  reads=[mvG[s].b, cst.b], writes=[rsG[s].b])
                            sc.op("act", lambda: A.activation(out=rsG[s][:], in_=rsG[s][:], func=AF.Exp, scale=-0.5), reads=[rsG[s].b], writes=[rsG[s].b])
                            yield
                            g3 = lambda t: t[:].rearrange("p (h d) -> p h d", h=4)
                            sc.op("dve", lambda: V.tensor_tensor(out=g3(gn1[s]), in0=g3(osb[s]),
                                                                 in1=fap(mvG[s], (slice(None), 0, slice(0, 1)), [[2, 4], [0, 64]]), op=ALU.subtract),
                                  reads=[osb[s].b, mvG[s].b], writes=[gn1[s].b])
                            sc.op("dve", lambda: V.tensor_tensor(out=g3(gn1[s]), in0=g3(gn1[s]), in1=fap(rsG[s], (slice(None), slice(0, 1)), [[1, 4], [0, 64]]),
                                                                 op=ALU.mult), reads=[gn1[s].b, rsG[s].b], writes=[gn1[s].b])
                            yield
                            sc.op("dve", lambda: V.tensor_tensor(out=gn1[s][:], in0=gn1[s][:], in1=tbp[:, 2, :], op=ALU.mult), reads=[gn1[s].b, tbp.b], writes=[gn1[s].b])
                            sc.op("dve", lambda: V.tensor_tensor(out=gn2[s][:], in0=gn1[s][:], in1=tbp[:, 3, :], op=ALU.add), reads=[gn1[s].b, tbp.b], writes=[gn2[s].b])
                            yield
                            sc.op("dve", lambda: V.scalar_tensor_tensor(out=yB[s][:], in0=gn2[s][:], scalar=0.5, in1=wgB[sb_][:], op0=ALU.mult, op1=ALU.mult),
                                  reads=[gn2[s].b, wgB[sb_].b], writes=[yB[s].b])
                            yield
                            transposes(ptr, lambda j: yB[s][:, j * 128:(j + 1) * 128], 2, [yB[s].b])
                            sc.op("act", lambda: A.copy(out=ytm[:, 2:4, s * 128:(s + 1) * 128], in_=ptr[:, 0:2, :]), reads=[ptr.b], writes=[ytm.b])
                            yield

                        def gmlp_chain(s, ytm, mp):
                            sb_ = mp * NS + s
                            gu = gu2[mp]
                            sc.op("act", lambda: A.activation(out=mvA[sb_][:, 1:2], in_=mvA[sb_][:, 1:2], func=AF.Ln, bias=cst[:, 0:1]),
                                  reads=[mvA[sb_].b, cst.b], writes=[mvA[sb_].b])
                            sc.op("act", lambda: A.activation(out=mvA[sb_][:, 1:2], in_=mvA[sb_][:, 1:2], func=AF.Exp, scale=-0.5), reads=[mvA[sb_].b], writes=[mvA[sb_].b])
                            yield
                            sc.op("dve", lambda: V.tensor_scalar(out=gv[sb_][:], in0=gv[sb_][:], scalar1=mvA[sb_][:, 0:1], scalar2=mvA[sb_][:, 1:2],
                                                                 op0=ALU.subtract, op1=ALU.mult), reads=[gv[sb_].b, mvA[sb_].b], writes=[gv[sb_].b])
                            yield
                            sc.op("dve", lambda: V.tensor_tensor(out=gv[sb_][:], in0=gv[sb_][:], in1=tbp[:, 0, :], op=ALU.mult), reads=[gv[sb_].b, tbp.b], writes=[gv[sb_].b])
                            sc.op("dve", lambda: V.tensor_tensor(out=vln[s][:], in0=gv[sb_][:], in1=tbp[:, 1, :], op=ALU.add), reads=[gv[sb_].b, tbp.b], writes=[vln[s].b])
                            yield

                            def f():
                                for h in range(4):
                                    i = PE.matmul(pmx[(h % 2) * 64:(h % 2) * 64 + 64, 256 + (h // 2) * 128:256 + (h // 2) * 128 + 128],
                                                  lhsT=vln[s][:, h * 64:(h + 1) * 64], rhs=wsT[:, h, :], start=True, stop=True)
                                return i
                            sc.op("pe", f, reads=[vln[s].b, wsT.b], writes=[pmx.b])
                            sc.op("dve", lambda: V.tensor_tensor(out=mixt[s][:], in0=pmx[:, 256:512].rearrange("p (a q) -> p a q", a=2), in1=bsT[:],
                                                                 op=ALU.add), reads=[pmx.b, bsT.b], writes=[mixt[s].b])
                            sc.op("dve", lambda: V.tensor_tensor(out=ytm[:, 0:2, s * 128:(s + 1) * 128], in0=mixt[s][:],
                                                                 in1=gu[:, :, s * 128:(s + 1) * 128], op=ALU.mult),
                                  reads=[mixt[s].b, gu.b], writes=[ytm.b])
                            yield

                        def lru_chain(s4, t0, mp):
                            xf, tgs, tas = xf2[mp], tgs2[mp], tas2[mp]
                            for c in range(2):
                                for z in range(2):
                                    zc = z * 2 + c
                                    adst = af[:] if z == 0 else s4[:, 2, c, :]
                                    adb = af.b if z == 0 else s4.b
                                    sc.op("act", lambda: A.activation(out=adst, in_=tgs[zc][:], func=AF.Exp, scale=pdv[:, 4 + zc:5 + zc],
                                                                      bias=pdv[:, 4 + zc:5 + zc]), reads=[tgs[zc].b, pdv.b], writes=[adb])
                                    sc.op("act", lambda: A.activation(out=e2[:], in_=tgs[zc][:], func=AF.Exp, scale=pdv[:, zc:zc + 1],
                                                                      bias=pdv[:, zc:zc + 1]), reads=[tgs[zc].b, pdv.b], writes=[e2.b])
                                    sc.op("act", lambda: A.activation(out=e2[:], in_=e2[:], func=AF.Ln, scale=-1.0, bias=cst[:, 1:2]),
                                          reads=[e2.b, cst.b], writes=[e2.b])
                                    sc.op("act", lambda: A.activation(out=e2[:], in_=e2[:], func=AF.Exp, scale=0.5, bias=cst[:, 2:3]),
                                          reads=[e2.b, cst.b], writes=[e2.b])
                                    yield
                                    bdst = bf_[:] if z == 0 else s4[:, 3, c, :]
                                    bdb = bf_.b if z == 0 else s4.b
                                    sc.op("dve", lambda: V.scalar_tensor_tensor(out=tas[zc][:], in0=tas[zc][:], scalar=1.0, in1=xf[:, c, :],
                                                                                op0=ALU.add, op1=ALU.mult), reads=[tas[zc].b, xf.b], writes=[tas[zc].b])
                                    sc.op("dve", lambda: V.tensor_tensor(out=bdst, in0=tas[zc][:], in1=e2[:], op=ALU.mult),
                                          reads=[tas[zc].b, e2.b], writes=[bdb])
                                    yield
                                    if z == 0:
                                        sc.op("dve", lambda: V.tensor_tensor_scan(out=s4[:, 0, c, :], data0=af[:], data1=bf_[:], initial=carry[:, c:c + 1],
                                                                                  op0=ALU.mult, op1=ALU.add), reads=[af.b, bf_.b, carry.b], writes=[s4.b])
                                        sc.op("dve", lambda: V.tensor_copy(out=carry[:, c:c + 1], in_=s4[:, 0, c, TA - 1:TA]), reads=[s4.b], writes=[carry.b])
                                        sc.op("dve", lambda: V.tensor_tensor(out=s4[:, 0, c, :], in0=s4[:, 0, c, :], in1=s4[:, 1, c, :], op=ALU.mult),
                                              reads=[s4.b], writes=[s4.b])
                                        yield
                            sc.dma("sp", [lambda q: q.dma_start(
                                out=lru_s[:, :, t0:t0 + TA].rearrange("a (c p) t -> p a c t", p=128), in_=s4[:])], reads=[s4.b])
                            yield

                        def d_chain(xtb, ytm):
                            for c in range(2):
                                pt = fm(xtb, 2560 + c * 128)
                                sc.op("act", lambda: A.copy(out=hd[:], in_=pt[:, 0:TA + 3]), reads=[pt.b], writes=[hd.b])
                                yield
                                pt = fm(xtb, 2304 + c * 128)
                                sc.op("dve", lambda: V.tensor_tensor(out=pd_[:], in0=pt[:, 0:TA + 3], in1=hd[:], op=ALU.mult),
                                      reads=[pt.b, hd.b], writes=[pd_.b])
                                yield
                                sc.op("act", lambda: A.activation(out=accd[:], in_=pd_[:, 1:1 + TA], func=AF.Identity, scale=pp[:, 22 + c * 3:23 + c * 3]),
                                      reads=[pd_.b, pp.b], writes=[accd.b])
                                yield
                                for k in range(1, 3):
                                    sc.op("dve", lambda k=k: V.scalar_tensor_tensor(
                                        out=accd[:], in0=pd_[:, 1 + k:1 + k + TA], scalar=pp[:, 22 + c * 3 + k:23 + c * 3 + k], in1=accd[:],
                                        op0=ALU.mult, op1=ALU.add), reads=[pd_.b, pp.b, accd.b], writes=[accd.b])
                                yield
                                pt = fm(xtb, 2048 + c * 128)
                                sc.op("dve", lambda: V.tensor_tensor(out=ytm[:, 4 + c, :], in0=pt[:, 2:2 + TA], in1=accd[:], op=ALU.mult),
                                      reads=[pt.b, accd.b], writes=[ytm.b])
                                yield

                        def stage1(m):
                            xtb = xt[m % 2]
                            mp = m % 2
                            s4 = st4[mp]
                            xf, tgs, tas, gu = xf2[mp], tgs2[mp], tas2[mp], gu2[mp]
                            for c in range(2):
                                pt = fm(xtb, 1536 + c * 128)
                                cx = cxs[c]
                                sc.op("act", lambda: A.copy(out=cx[:], in_=pt[:, 0:TA + 3]), reads=[pt.b], writes=[cx.b])
                                sc.op("dve", lambda: V.tensor_scalar(out=xf[:, c, :], in0=cx[:, 0:TA], scalar1=pp[:, c * 4:c * 4 + 1],
                                                                      scalar2=pp[:, 8 + c:9 + c], op0=ALU.mult, op1=ALU.add),
                                      reads=[cx.b, pp.b], writes=[xf.b])
                                for k in range(1, 4):
                                    sc.op("dve", lambda k=k: V.scalar_tensor_tensor(
                                        out=xf[:, c, :], in0=cx[:, k:k + TA], scalar=pp[:, c * 4 + k:c * 4 + k + 1], in1=xf[:, c, :],
                                        op0=ALU.mult, op1=ALU.add), reads=[cx.b, pp.b, xf.b], writes=[xf.b])
                                sc.op("act", lambda: A.copy(out=xfb[:, c, :], in_=xf[:, c, :]), reads=[xf.b], writes=[xfb.b])
                            yield
                            for c in range(2):
                                pt = fm(xtb, 1792 + c * 128)
                                sc.op("act", lambda: A.activation(out=s4[:, 1, c, :], in_=pt[:, 2:2 + TA], func=AF.Gelu_apprx_tanh),
                                      reads=[pt.b], writes=[s4.b])
                                pt = fm(xtb, 0 + c * 128)
                                sc.op("act", lambda: A.activation(out=gu[:, c, :], in_=pt[:, 2:2 + TA], func=AF.Gelu_apprx_tanh),
                                      reads=[pt.b], writes=[gu.b])
                            yield
                            for c in range(2):
                                for z in range(2):
                                    zc = z * 2 + c
                                    for t_, dst, boff in ((0, tgs[zc], 8), (1, tas[zc], 12)):
                                        pt = pz[pzi[0] % 2]
                                        pzi[0] += 1
                                        sc.op("pe", lambda: PE.matmul(pt[:, 0:TA], lhsT=wbd[:, (z * 2 + t_) * 2 + c, :], rhs=xfb[:, c, :],
                                                                      start=True, stop=True), reads=[wbd.b, xfb.b], writes=[pt.b])
                                        sc.op("act", lambda: A.activation(
                                            out=dst[:], in_=pt[:, 0:TA], func=AF.Tanh, scale=0.5, bias=pdv[:, boff + zc:boff + 1 + zc]),
                                            reads=[pt.b, pdv.b], writes=[dst.b])
                            yield
                            for s in range(NS):
                                tc = 2 + s * 128
                                sb_ = mp * NS + s
                                tok_mm(ptm1, xtb, tc, 256, 256)
                                sc.op("act", lambda: A.activation(out=gv[sb_][:], in_=ptm1[:, 0:256], func=AF.Gelu_apprx_tanh), reads=[ptm1.b], writes=[gv[sb_].b])
                                sc.op("dve", lambda: V.bn_stats(out=stA[sb_][:], in_=gv[sb_][:]), reads=[gv[sb_].b], writes=[stA[sb_].b])
                                sc.op("dve", lambda: V.bn_aggr(out=mvA[sb_][:], in_=stA[sb_][:]), reads=[stA[sb_].b], writes=[mvA[sb_].b])
                                tok_mm(ptm1, xtb, tc, 1024, 512)
                                sc.op("act", lambda: A.copy(out=vbs[sb_][:], in_=ptm1[:, 0:256]), reads=[ptm1.b], writes=[vbs[sb_].b])
                                sc.op("act", lambda: A.activation(out=tgB[sb_][:], in_=ptm1[:, 256:512], func=AF.Tanh, scale=0.5), reads=[ptm1.b], writes=[tgB[sb_].b])
                                sc.op("dve", lambda: V.scalar_tensor_tensor(out=wgB[sb_][:], in0=tgB[sb_][:], scalar=1.0, in1=ptm1[:, 256:512],
                                                                            op0=ALU.add, op1=ALU.mult), reads=[tgB[sb_].b, ptm1.b], writes=[wgB[sb_].b])
                            yield

                        load_xt(0)
                        for _ in stage1(0):
                            pass
                        pendingA = []

                        def flushA():
                            for item in list(pendingA):
                                gs, fn = item
                                if not any(g in chains for g in gs):
                                    fn()
                                    pendingA.remove(item)

                        def store_tile(ytm, t0):
                            sc.dma("sp", [
                                lambda q: q.dma_start(out=yT_s[0:512, t0:t0 + TA].rearrange("(k p) t -> p k t", p=128), in_=ytm[:, 0:4, :]),
                                lambda q: q.dma_start(out=yT_s[768:1024, t0:t0 + TA].rearrange("(k p) t -> p k t", p=128), in_=ytm[:, 4:6, :]),
                            ], reads=[ytm.b])

                        for m in range(NM):
                            xtb = xt[m % 2]
                            mp = m % 2
                            t0 = m * TA
                            s4 = st4[mp]
                            ytm = yt[mp]
                            mine = [lru_chain(s4, t0, mp)]
                            for s in range(NS):
                                mine.append(ret_chain(s, m * NS + s, xtb, ytm, mp))
                            for s in range(NS):
                                mine.append(gmlp_chain(s, ytm, mp))
                            mine.append(d_chain(xtb, ytm))
                            chains.extend(mine)
                            pendingA.append((mine, lambda ytm=ytm, t0=t0: store_tile(ytm, t0)))
                            for r in range(1, S1_ROUND + 1):
                                pump()
                                flushA()
                                if r == 4 and m + 1 < NM:
                                    load_xt(m + 1)
                            if m + 1 < NM:
                                for _ in stage1(m + 1):
                                    pass
                        while chains or pendingA:
                            pump()
                            flushA()
                    sc.barrier()
                    stop('phaseA')

                def ln_chain(T, po, resid, lnp, dst_dram, row0, stT, scol, want_T, ptT):
                    s_t, stL, mvL, nmr, xn, xbf = T
                    for hf_ in range(2):
                        sc.op("dve", lambda hf_=hf_: V.scalar_tensor_tensor(
                            out=s_t[:, hf_ * 512:(hf_ + 1) * 512], in0=resid[:, hf_ * 512:(hf_ + 1) * 512], scalar=ALPHA,
                            in1=po[:, hf_ * 512:(hf_ + 1) * 512], op0=ALU.mult, op1=ALU.add), reads=[resid.b, po.b], writes=[s_t.b])
                        yield

                    def f():
                        for hf_ in range(2):
                            i = V.bn_stats(out=stL[:, hf_, :], in_=s_t[:, hf_ * 512:(hf_ + 1) * 512])
                        return i
                    sc.op("dve", f, reads=[s_t.b], writes=[stL.b])
                    yield
                    sc.op("dve", lambda: V.bn_aggr(out=mvL[:], in_=stL[:]), reads=[stL.b], writes=[mvL.b])
                    yield
                    sc.op("act", lambda: A.activation(out=mvL[:, 1:2], in_=mvL[:, 1:2], func=AF.Ln, bias=cst[:, 0:1]), reads=[mvL.b, cst.b], writes=[mvL.b])
                    sc.op("act", lambda: A.activation(out=mvL[:, 1:2], in_=mvL[:, 1:2], func=AF.Exp, scale=-0.5), reads=[mvL.b], writes=[mvL.b])
                    yield
                    sc.op("dve", lambda: V.scalar_tensor_tensor(out=nmr[:], in0=mvL[:, 0:1], scalar=-1.0, in1=mvL[:, 1:2], op0=ALU.mult, op1=ALU.mult),
                          reads=[mvL.b], writes=[nmr.b])
                    yield
                    sc.op("act", lambda: A.activation(out=xn[:], in_=s_t[:], func=AF.Identity, scale=mvL[:, 1:2], bias=nmr[:]),
                          reads=[s_t.b, mvL.b, nmr.b], writes=[xn.b])
                    yield
                    sc.op("pool", lambda: G.tensor_tensor(out=xn[:], in0=xn[:], in1=lnp[:, 0, :], op=ALU.mult), reads=[xn.b, lnp.b], writes=[xn.b])
                    yield
                    sc.op("pool", lambda: G.tensor_tensor(out=xn[:], in0=xn[:], in1=lnp[:, 1, :], op=ALU.add), reads=[xn.b, lnp.b], writes=[xn.b])
                    yield
                    sc.dma("sp", [lambda q: q.dma_start(out=dst_dram[row0:row0 + 128, :], in_=xn[:])], reads=[xn.b])
                    if want_T:
                        sc.op("act", lambda: A.copy(out=xbf[:], in_=xn[:]), reads=[xn.b], writes=[xbf.b])
                        yield
                        transposes(ptT, lambda j: xbf[:, j * 128:(j + 1) * 128], 8, [xbf.b])
                        sc.op("dve", lambda: V.tensor_copy(out=stT[:, :, scol:scol + 128], in_=ptT[:]), reads=[ptT.b], writes=[stT.b])
                    yield

                def ln_tiles(es, tag, n):
                    out = []
                    for i in range(n):
                        s_t = sb(es, "s_t%s%d" % (tag, i), [128, D])
                        stL = sb(es, "stL%s%d" % (tag, i), [128, 2, 6])
                        mvL = sb(es, "mvL%s%d" % (tag, i), [128, 2])
                        nmr = sb(es, "nmr%s%d" % (tag, i), [128, 1])
                        xn = sb(es, "xn%s%d" % (tag, i), [128, D])
                        xbf = sb(es, "xbf%s%d" % (tag, i), [128, D], BF16)
                        for t_ in (stL, mvL, nmr):
                            t_.b.strict = True
                        out.append((s_t, stL, mvL, nmr, xn, xbf))
                    return out

                with ExitStack() as esW:
                    wg = sb(esW, "wg", [128, 8, DFF], BF16)
                    wu = sb(esW, "wu", [128, 8, DFF], BF16)

                    with ExitStack() as es:
                        NB = 3
                        NL = 4
                        wout = sb(es, "wout", [128, 8, D], BF16)
                        sc.dma("pool", [lambda q, k=k: q.dma_start(out=wout[:, k, :], in_=w_out_d[l, k * 128:(k + 1) * 128, :]) for k in range(8)],
                               writes=[wout.b])
                        sc.dma("pool", [lambda q, k=k: q.dma_start(out=wg[:, k, :], in_=wg_d[l, k * 128:(k + 1) * 128, :]) for k in range(8)], writes=[wg.b])
                        sc.dma("pool", [lambda q, k=k: q.dma_start(out=wu[:, k, :], in_=wu_d[l, k * 128:(k + 1) * 128, :]) for k in range(8)], writes=[wu.b])
                        lnp = sb(es, "lnpB", [128, 2, D])
                        sc.dma("sp", [lambda q: q.dma_start(out=lnp[:].rearrange("p a d -> p (a d)"),
                                                            in_=lnp_d[l, 0:2 * D].partition_broadcast(128))], writes=[lnp.b])
                        TBm = 256
                        yt2 = [sb(es, "yt2_%d" % i, [128, 8, TBm], BF16) for i in range(2)]
                        l4 = [sb(es, "l4_%d" % i, [128, 4, 2, TBm]) for i in range(2)]
                        hb = sb(es, "hb", [128, TBm])
                        carryB = sb(es, "carryB", [128, 2])
                        carryB.b.strict = True
                        xr = [sb(es, "xr%d" % i, [128, D]) for i in range(NB)]
                        LT = ln_tiles(es, "B", NL)
                        stT = [sb(es, "stT%d" % i, [128, 8, TBm], BF16) for i in range(2)]
                        po = [ps(es, "poB%d" % i, [128, D]) for i in range(NB)]
                        ptT = [ps(es, "ptTB%d" % i, [128, 8, 128], BF16) for i in range(2)]
                        sc.op("dve", lambda: V.memset(carryB[:], 0.0), writes=[carryB.b])
                        NMB = S // TBm

                        def load_B(m):
                            t0 = m * TBm
                            y = yt2[m % 2]
                            l_ = l4[m % 2]
                            sc.dma("sp", [
                                lambda q: q.dma_start(out=y[:, 0:4, :], in_=yT_s[0:512, t0:t0 + TBm].rearrange("(k p) t -> p k t", p=128)),
                                lambda q: q.dma_start(out=y[:, 6:8, :], in_=yT_s[768:1024, t0:t0 + TBm].rearrange("(k p) t -> p k t", p=128)),
                            ], writes=[y.b])
                            sc.dma("sp", [lambda q: q.dma_start(out=l_[:], in_=lru_s[:, :, t0:t0 + TBm].rearrange("a (c p) t -> p a c t", p=128))],
                                   writes=[l_.b])

                        def rev(ap2):
                            return AP(ap2.tensor, ap2.offset + (ap2.ap[-1][1] - 1) * ap2.ap[-1][0], [list(ap2.ap[0]), [-ap2.ap[-1][0], ap2.ap[-1][1]]])

                        def stT_store(sT_, t0):
                            yield
                            sc.dma("sp", [lambda q: q.dma_start(
                                out=x1T_s[:, t0:t0 + TBm].rearrange("(k p) t -> p k t", p=128), in_=sT_[:])], reads=[sT_.b])
                            yield

                        load_B(NMB - 1)
                        ci = 0
                        pending = []

                        def flush_stores():
                            for item in list(pending):
                                gs, fn = item
                                if not any(g in chains for g in gs):
                                    fn()
                                    pending.remove(item)

                        def pumpB(n=1):
                            for _ in range(n):
                                pump()
                                flush_stores()

                        for m in reversed(range(NMB)):
                            if m > 0:
                                load_B(m - 1)
                            t0 = m * TBm
                            y = yt2[m % 2]
                            l_ = l4[m % 2]
                            sT_ = stT[m % 2]
                            while len(pending) > 1:
                                pumpB()
                            for c in range(2):
                                sc.op("dve", lambda c=c, l_=l_: V.tensor_tensor_scan(out=rev(hb[:]), data0=rev(l_[:, 2, c, :]), data1=rev(l_[:, 3, c, :]),
                                                                                     initial=carryB[:, c:c + 1], op0=ALU.mult, op1=ALU.add),
                                      reads=[l_.b, carryB.b], writes=[hb.b])
                                sc.op("dve", lambda c=c: V.tensor_copy(out=carryB[:, c:c + 1], in_=hb[:, 0:1]), reads=[hb.b], writes=[carryB.b])
                                sc.op("dve", lambda c=c, l_=l_: V.tensor_tensor(out=hb[:], in0=hb[:], in1=l_[:, 1, c, :], op=ALU.mult),
                                      reads=[hb.b, l_.b], writes=[hb.b])
                                sc.op("dve", lambda c=c, l_=l_, y=y: V.tensor_tensor(out=y[:, 4 + c, :], in0=hb[:], in1=l_[:, 0, c, :], op=ALU.add),
                                      reads=[hb.b, l_.b], writes=[y.b])
                            mt = []
                            for s in reversed(range(TBm // 128)):
                                row0 = t0 + s * 128
                                while len(chains) > NL - 1:
                                    pumpB()
                                xr_ = xr[ci % NB]
                                po_ = po[ci % NB]
                                T_ = LT[ci % NL]
                                pt_ = ptT[ci % 2]
                                ci += 1
                                sc.dma("sp", [lambda q, xr_=xr_, row0=row0: q.dma_start(out=xr_[:], in_=x_res[row0:row0 + 128, :])], writes=[xr_.b])

                                def f(y=y, s=s, po_=po_):
                                    for hf_ in range(2):
                                        for k in range(8):
                                            i = PE.matmul(po_[:, hf_ * 512:(hf_ + 1) * 512], lhsT=y[:, k, s * 128:(s + 1) * 128],
                                                          rhs=wout[:, k, hf_ * 512:(hf_ + 1) * 512], start=(k == 0), stop=(k == 7))
                                    return i
                                sc.op("pe", f, reads=[y.b, wout.b], writes=[po_.b])
                                g = ln_chain(T_, po_, xr_, lnp, x1_s, row0, sT_, s * 128, True, pt_)
                                next(g)
                                next(g)
                                chains.append(g)
                                mt.append(g)
                                pumpB(4)
                            pending.append((mt, lambda sT_=sT_, t0=t0: sc.dma("sp", [lambda q: q.dma_start(
                                out=x1T_s[:, t0:t0 + TBm].rearrange("(k p) t -> p k t", p=128), in_=sT_[:])], reads=[sT_.b])))
                        while chains or pending:
                            pumpB()
                        drain()
                        sc.barrier()
                        stop('phaseB')

                    with ExitStack() as es:
                        wd = sb(es, "wd", [128, 22, D], BF16)
                        for k0 in range(0, 22, 6):
                            sc.dma("pool", [lambda q, k=k: q.dma_start(out=wd[:, k, :], in_=wd_d[l, k * 128:(k + 1) * 128, :])
                                            for k in range(k0, min(22, k0 + 6))], writes=[wd.b])
                        lnp = sb(es, "lnpC", [128, 2, D])
                        sc.dma("sp", [lambda q: q.dma_start(out=lnp[:].rearrange("p a d -> p (a d)"),
                                                            in_=lnp_d[l, 2 * D:4 * D].partition_broadcast(128))], writes=[lnp.b])
                        xt1 = [sb(es, "xt1_%d" % i, [128, 8, TC], BF16) for i in range(2)]
                        hh_ = sb(es, "hh", [128, 22, TC], BF16)
                        sg = [sb(es, "sg%d" % i, [128, TC]) for i in range(2)]
                        xr = [sb(es, "xrC%d" % i, [128, D]) for i in range(2)]
                        LT = ln_tiles(es, "C", 2)
                        stT = [sb(es, "stTC%d" % i, [128, 8, TC], BF16) for i in range(2)]
                        pg = [ps(es, "pg%d" % i, [128, 2, TC]) for i in range(3)]
                        po = [ps(es, "poC%d" % i, [128, D]) for i in range(2)]
                        ptT = [ps(es, "ptTC%d" % i, [128, 8, 128], BF16) for i in range(1)]
                        NMC = S // TC

                        def load_C(m):
                            b = xt1[m % 2]
                            t0 = m * TC
                            sc.dma("sp", [lambda q: q.dma_start(out=b[:], in_=x1T_s[:, t0:t0 + TC].rearrange("(k p) t -> p k t", p=128))], writes=[b.b])

                        def store_xT(sT_, t0):
                            sc.dma("sp", [lambda q: q.dma_start(
                                out=xT_s[:, 2 + t0:2 + t0 + TC].rearrange("(k p) t -> p k t", p=128), in_=sT_[:])], reads=[sT_.b])

                        load_C(0)
                        gi = 0
                        pend = None
                        for m in range(NMC):
                            if m + 1 < NMC:
                                load_C(m + 1)
                            t0 = m * TC
                            xb_ = xt1[m % 2]
                            sT_ = stT[m % 2]
                            for fch in range(22):
                                p_ = pg[gi % 3]
                                sg_ = sg[gi % 2]
                                gi += 1

                                def f(p_=p_, fch=fch, xb_=xb_):
                                    for k in range(8):
                                        PE.matmul(p_[:, 0, :], lhsT=wg[:, k, fch * 128:(fch + 1) * 128], rhs=xb_[:, k, :], start=(k == 0), stop=(k == 7))
                                    for k in range(8):
                                        i = PE.matmul(p_[:, 1, :], lhsT=wu[:, k, fch * 128:(fch + 1) * 128], rhs=xb_[:, k, :], start=(k == 0), stop=(k == 7))
                                    return i
                                sc.op("pe", f, reads=[wg.b, wu.b, xb_.b], writes=[p_.b])
                                sc.op("act", lambda p_=p_, sg_=sg_: A.activation(out=sg_[:], in_=p_[:, 0, :], func=AF.Silu), reads=[p_.b], writes=[sg_.b])
                                sc.op("dve", lambda p_=p_, sg_=sg_, fch=fch: V.tensor_tensor(out=hh_[:, fch, :], in0=sg_[:], in1=p_[:, 1, :], op=ALU.mult),
                                      reads=[sg_.b, p_.b], writes=[hh_.b])
                                if fch >= 2:
                                    pump(1)
                            drain()
                            if pend is not None:
                                store_xT(*pend)
                                pend = None
                            for s in range(TC // 128):
                                row0 = t0 + s * 128
                                xr_ = xr[s]
                                po_ = po[s]
                                sc.dma("sp", [lambda q, xr_=xr_, row0=row0: q.dma_start(out=xr_[:], in_=x1_s[row0:row0 + 128, :])], writes=[xr_.b])

                                def f(s=s, po_=po_):
                                    for hf_ in range(2):
                                        for k in range(22):
                                            i = PE.matmul(po_[:, hf_ * 512:(hf_ + 1) * 512], lhsT=hh_[:, k, s * 128:(s + 1) * 128],
                                                          rhs=wd[:, k, hf_ * 512:(hf_ + 1) * 512], start=(k == 0), stop=(k == 21))
                                    return i
                                sc.op("pe", f, reads=[hh_.b, wd.b], writes=[po_.b])
                                chains.append(ln_chain(LT[s], po_, xr_, lnp, x_dst, row0, sT_, s * 128, not last, ptT[0]))
                            if not last:
                                pend = (sT_, t0)
                        drain()
                        if pend is not None:
                            store_xT(*pend)
                        sc.barrier()
        except _Stop:
            pass
        sc.barrier()
    return nc


def _consts():
    h = np.arange(4, dtype=np.float64)
    lg = np.log1p(-np.exp2(-5.0 - h))
    idx = np.arange(128, dtype=np.float64)
    mask = np.exp(lg[None, :, None] * np.abs(idx[:, None, None] - idx[None, None, :])) * 0.125
    dec = np.zeros((128, 16), np.float64)
    dec[:, 0:4] = np.exp(lg[None, :] * (127 - idx)[:, None]) * 0.125
    dec[:, 4:8] = np.exp(lg[None, :] * idx[:, None]) * 0.125
    qf = np.exp(lg[None, :] * (idx + 1.0)[:, None])
    qb = np.exp(lg[None, :] * (128 - idx)[:, None])
    dec[:, 8:16] = np.stack([qf, qb], axis=2).reshape(128, 8)
    cdec = np.repeat(np.exp(lg * 128), 64)[None, :].repeat(128, 0)
    invf = (10000.0 ** (-np.arange(0, 64, 2, dtype=np.float32) / 64)).astype(np.float32)[None, :].repeat(128, 0)
    return {
        "ident": np.eye(128, dtype=np.float32),
        "maskT": mask.reshape(128, 512).astype(np.float32),
        "dec": dec.astype(np.float32),
        "cdec": cdec.astype(np.float32),
        "invf": np.ascontiguousarray(invf),
    }


def _host_layout(inp):
    f = lambda k: np.asarray(inp[k], dtype=np.float32)
    cw, cb = f("lru_conv_w"), f("lru_conv_b")
    ba, bx, lam, sw = f("lru_ba"), f("lru_bx"), f("lru_lambda"), f("sc_conv_w")
    pp = np.zeros((L, 128, 28), np.float32)
    for c in range(2):
        sl = slice(c * 128, (c + 1) * 128)
        for k in range(4):
            pp[:, :, c * 4 + k] = cw[:, k, sl]
        pp[:, :, 8 + c] = cb[:, sl]
        for z in range(2):
            pp[:, :, 10 + z * 2 + c] = ba[:, z, sl]
            pp[:, :, 14 + z * 2 + c] = bx[:, z, sl]
            pp[:, :, 18 + z * 2 + c] = lam[:, z, sl]
        for k in range(3):
            pp[:, :, 22 + c * 3 + k] = sw[:, k, sl]
    wa, wx = f("lru_wa"), f("lru_wx")
    wbd = np.zeros((L, 8, 128, 128), np.float32)
    for z in range(2):
        for t, wsrc in enumerate((wa, wx)):
            for c in range(2):
                e = (z * 2 + t) * 2 + c
                wbd[:, e, 0:64, 0:64] = wsrc[:, z, 2 * c]
                wbd[:, e, 64:128, 64:128] = wsrc[:, z, 2 * c + 1]
    tb = np.stack([f("gmlp_ln_g"), f("gmlp_ln_b"), f("ret_gn_g"), f("ret_gn_b")], axis=1).reshape(L, 4 * W)
    lnp = np.stack([f("ln1_g"), f("ln1_b"), f("ln2_g"), f("ln2_b")], axis=1).reshape(L, 4 * D)
    shared = {
        "w_in": f("w_in"), "w_out": f("w_out"), "ffn_wg": f("ffn_wg"), "ffn_wu": f("ffn_wu"), "ffn_wd": f("ffn_wd"),
        "pp": pp, "wbd": wbd, "tb": np.ascontiguousarray(tb), "gmlp_bs": f("gmlp_bs"), "gmlp_ws": f("gmlp_ws"),
        "lnp": np.ascontiguousarray(lnp),
    }
    shared.update(_consts())
    return shared


def kernel(**inputs):
    x = np.asarray(inputs["x"], dtype=np.float32)
    pos = np.asarray(inputs["positions"], dtype=np.int32)
    shared = _host_layout(inputs)
    nc = build_program()
    in_maps = []
    for b in range(NCORES):
        m = dict(shared)
        m["x"] = np.ascontiguousarray(x[b])
        m["pos"] = np.ascontiguousarray(pos[b].reshape(NCH, 128).T)
        in_maps.append(m)
    res = run_bass_kernel_spmd(nc, in_maps, core_ids=list(range(NCORES)))
    return np.stack([np.asarray(r["out"], dtype=np.float32) for r in res.results], axis=0)
```
